# Optimizing a Trainium2 kernel written in Bass

```python
import jax, jax.numpy as jnp
from jax import lax
import numpy as np

D_MODEL = 1024
BATCH = 8
SEQ = 4096
DEPTH = 2

N_MEM = 256
DA_HEADS = 4
DA_HEAD_DIM = 64
DA_V_DIM = 2 * DA_HEAD_DIM
DA_WIDTH = DA_HEADS * DA_V_DIM
LRU_WIDTH = 512
LRU_BLOCKS = 8
LRU_BLOCK = LRU_WIDTH // LRU_BLOCKS
LRU_C = 8.0
CONV_WIDTH = 4
CONV_LEFT = 2
CONV_RIGHT = 1
MIX_WIDTH = DA_WIDTH + LRU_WIDTH
IN_WIDTH = 3 * DA_WIDTH + 2 * LRU_WIDTH
ROPE_THETA = 500000.0
ROPE_DIM = DA_HEAD_DIM // 4
X_HEADS = 4
X_HEAD_DIM = D_MODEL // X_HEADS
D_FF = 4 * D_MODEL
Q_BLOCK = 128
EPS = 1e-6

kernel_name = "hybrid_diffattn_rglru_memory_encoder"


def rms_norm(x, g):
    xf = x.astype(jnp.float32)
    y = xf * lax.rsqrt(jnp.mean(xf * xf, axis=-1, keepdims=True) + EPS)
    return (y * g.astype(jnp.float32)).astype(x.dtype)


def rope_tables(positions):
    inv_freq = jnp.power(jnp.float32(ROPE_THETA),
                         -jnp.arange(0, ROPE_DIM, 2, dtype=jnp.float32) / ROPE_DIM)
    ang = positions.astype(jnp.float32)[..., None] * inv_freq
    return jnp.cos(ang)[:, :, None, None, :], jnp.sin(ang)[:, :, None, None, :]


def partial_rope(t, cos, sin):
    half = ROPE_DIM // 2
    c = cos.astype(t.dtype)
    s = sin.astype(t.dtype)
    t1 = t[..., :half]
    t2 = t[..., half:ROPE_DIM]
    return jnp.concatenate([t1 * c - t2 * s, t2 * c + t1 * s, t[..., ROPE_DIM:]], axis=-1)


def diff_attention(q, k, v, lam):
    B, S = q.shape[0], q.shape[1]
    nb = S // Q_BLOCK
    scale = DA_HEAD_DIM ** -0.5
    k1 = k[:, :, :, 0]
    k2 = k[:, :, :, 1]
    qb = q.reshape(B, nb, Q_BLOCK, DA_HEADS, 2, DA_HEAD_DIM).transpose(1, 0, 2, 3, 4, 5)

    def block(qblk):
        s1 = jnp.einsum('bqhd,bkhd->bhqk', qblk[:, :, :, 0], k1,
                        preferred_element_type=jnp.float32) * scale
        s2 = jnp.einsum('bqhd,bkhd->bhqk', qblk[:, :, :, 1], k2,
                        preferred_element_type=jnp.float32) * scale
        p = jax.nn.softmax(s1, axis=-1) - lam * jax.nn.softmax(s2, axis=-1)
        return jnp.einsum('bhqk,bkhe->bqhe', p.astype(v.dtype), v)

    o = lax.map(block, qb)
    return o.transpose(1, 0, 2, 3, 4).reshape(B, S, DA_HEADS, DA_V_DIM)


def centred_dwconv(x, w, b):
    S = x.shape[1]
    xp = jnp.pad(x, ((0, 0), (CONV_LEFT, CONV_RIGHT), (0, 0)))
    y = b
    for j in range(CONV_WIDTH):
        y = y + xp[:, j:j + S] * w[j]
    return y


def block_diag(x, w, b):
    B, S = x.shape[0], x.shape[1]
    xb = x.reshape(B, S, LRU_BLOCKS, LRU_BLOCK)
    return (jnp.einsum('bsni,nij->bsnj', xb, w) + b).reshape(B, S, LRU_WIDTH)


def _linrec(e1, e2):
    a1, b1 = e1
    a2, b2 = e2
    return a1 * a2, a2 * b1 + b2


def rg_lru(x, w_r, b_r, w_i, b_i, a_param, reverse):
    r = jax.nn.sigmoid(block_diag(x, w_r, b_r).astype(jnp.float32))
    i = jax.nn.sigmoid(block_diag(x, w_i, b_i).astype(jnp.float32))
    log_a = -LRU_C * r * jax.nn.softplus(-a_param.astype(jnp.float32))
    a = jnp.exp(log_a)
    u = x.astype(jnp.float32) * i * jnp.sqrt(-jnp.expm1(2.0 * log_a))
    _, h = lax.associative_scan(_linrec, (a, u), axis=1, reverse=reverse)
    return h.astype(x.dtype)


def setup_inputs(seed: int = 0) -> dict:
    key = jax.random.key(seed)
    ks = iter(jax.random.split(key, 64))
    f32 = jnp.float32

    def nrm(shape, scale):
        return jax.random.normal(next(ks), shape, f32) * scale

    def gain(shape):
        return 1.0 + 0.02 * jax.random.normal(next(ks), shape, f32)

    u = jax.random.uniform(next(ks), (DEPTH, 2, LRU_WIDTH), f32, 0.9, 0.999)
    a0 = u ** (1.0 / LRU_C)
    lru_a_param = jnp.log(a0) - jnp.log1p(-a0)

    offs = jax.random.randint(next(ks), (BATCH, 1), 0, 1024, dtype=jnp.int32)
    positions = jnp.arange(SEQ, dtype=jnp.int32)[None, :] + offs

    return {
        "x": nrm((BATCH, SEQ, D_MODEL), 1.0),
        "mem": nrm((BATCH, N_MEM, D_MODEL), 1.0),
        "positions": positions,
        "g_mix": gain((DEPTH, D_MODEL)),
        "w_in": nrm((DEPTH, D_MODEL, IN_WIDTH), D_MODEL ** -0.5),
        "da_lq1": nrm((DEPTH, DA_HEAD_DIM), 0.1),
        "da_lk1": nrm((DEPTH, DA_HEAD_DIM), 0.1),
        "da_lq2": nrm((DEPTH, DA_HEAD_DIM), 0.1),
        "da_lk2": nrm((DEPTH, DA_HEAD_DIM), 0.1),
        "da_subln_g": gain((DEPTH, DA_V_DIM)),
        "lru_conv_w": nrm((DEPTH, CONV_WIDTH, LRU_WIDTH), CONV_WIDTH ** -0.5),
        "lru_conv_b": nrm((DEPTH, LRU_WIDTH), 0.01),
        "lru_w_r": nrm((DEPTH, 2, LRU_BLOCKS, LRU_BLOCK, LRU_BLOCK), LRU_BLOCK ** -0.5),
        "lru_b_r": nrm((DEPTH, 2, LRU_BLOCKS, LRU_BLOCK), 0.01),
        "lru_w_i": nrm((DEPTH, 2, LRU_BLOCKS, LRU_BLOCK, LRU_BLOCK), LRU_BLOCK ** -0.5),
        "lru_b_i": nrm((DEPTH, 2, LRU_BLOCKS, LRU_BLOCK), 0.01),
        "lru_a_param": lru_a_param,
        "lru_norm_g": gain((DEPTH, LRU_WIDTH)),
        "w_out": nrm((DEPTH, MIX_WIDTH, D_MODEL), MIX_WIDTH ** -0.5),
        "g_xq": gain((DEPTH, D_MODEL)),
        "g_mem": gain((DEPTH, D_MODEL)),
        "xa_wq": nrm((DEPTH, D_MODEL, D_MODEL), D_MODEL ** -0.5),
        "xa_wk": nrm((DEPTH, D_MODEL, D_MODEL), D_MODEL ** -0.5),
        "xa_wv": nrm((DEPTH, D_MODEL, D_MODEL), D_MODEL ** -0.5),
        "xa_wo": nrm((DEPTH, D_MODEL, D_MODEL), D_MODEL ** -0.5),
        "g_mlp": gain((DEPTH, D_MODEL)),
        "mlp_w1": nrm((DEPTH, D_MODEL, D_FF), D_MODEL ** -0.5),
        "mlp_w2": nrm((DEPTH, D_FF, D_MODEL), D_FF ** -0.5),
        "g_final": gain((D_MODEL,)),
    }


def reference(x, mem, positions, g_mix, w_in, da_lq1, da_lk1, da_lq2, da_lk2, da_subln_g,
              lru_conv_w, lru_conv_b, lru_w_r, lru_b_r, lru_w_i, lru_b_i, lru_a_param,
              lru_norm_g, w_out, g_xq, g_mem, xa_wq, xa_wk, xa_wv, xa_wo,
              g_mlp, mlp_w1, mlp_w2, g_final):
    B, S = x.shape[0], x.shape[1]
    M = mem.shape[1]
    cos, sin = rope_tables(positions)

    for l in range(DEPTH):
        h = rms_norm(x, g_mix[l])
        z = h @ w_in[l]
        q = z[..., :DA_WIDTH].reshape(B, S, DA_HEADS, 2, DA_HEAD_DIM)
        k = z[..., DA_WIDTH:2 * DA_WIDTH].reshape(B, S, DA_HEADS, 2, DA_HEAD_DIM)
        v = z[..., 2 * DA_WIDTH:3 * DA_WIDTH].reshape(B, S, DA_HEADS, DA_V_DIM)
        xb = z[..., 3 * DA_WIDTH:3 * DA_WIDTH + LRU_WIDTH]
        gb = z[..., 3 * DA_WIDTH + LRU_WIDTH:]

        q = partial_rope(q, cos, sin)
        k = partial_rope(k, cos, sin)
        lam_init = 0.8 - 0.6 * float(np.exp(-0.3 * l))
        lam = (jnp.exp(jnp.sum(da_lq1[l].astype(jnp.float32) * da_lk1[l].astype(jnp.float32)))
               - jnp.exp(jnp.sum(da_lq2[l].astype(jnp.float32) * da_lk2[l].astype(jnp.float32)))
               + lam_init)
        o = diff_attention(q, k, v, lam)
        o = (rms_norm(o, da_subln_g[l]) * (1.0 - lam_init)).reshape(B, S, DA_WIDTH)

        c = centred_dwconv(xb, lru_conv_w[l], lru_conv_b[l])
        h_f = rg_lru(c, lru_w_r[l, 0], lru_b_r[l, 0], lru_w_i[l, 0], lru_b_i[l, 0],
                     lru_a_param[l, 0], False)
        h_b = rg_lru(c, lru_w_r[l, 1], lru_b_r[l, 1], lru_w_i[l, 1], lru_b_i[l, 1],
                     lru_a_param[l, 1], True)
        r = rms_norm((h_f + h_b) * jax.nn.gelu(gb), lru_norm_g[l])

        x = x + jnp.concatenate([o, r], axis=-1) @ w_out[l]

        hq = rms_norm(x, g_xq[l])
        m = rms_norm(mem, g_mem[l])
        qx = (hq @ xa_wq[l]).reshape(B, S, X_HEADS, X_HEAD_DIM)
        kx = (m @ xa_wk[l]).reshape(B, M, X_HEADS, X_HEAD_DIM)
        vx = (m @ xa_wv[l]).reshape(B, M, X_HEADS, X_HEAD_DIM)
        s = jnp.einsum('bshd,bmhd->bhsm', qx, kx,
                       preferred_element_type=jnp.float32) * (X_HEAD_DIM ** -0.5)
        p = jax.nn.softmax(s, axis=-1).astype(vx.dtype)
        ox = jnp.einsum('bhsm,bmhd->bshd', p, vx).reshape(B, S, D_MODEL)
        x = x + ox @ xa_wo[l]

        hm = rms_norm(x, g_mlp[l])
        x = x + jnp.square(jax.nn.relu(hm @ mlp_w1[l])) @ mlp_w2[l]

    return rms_norm(x, g_final)
```

```python
from contextlib import ExitStack
import numpy as np
import concourse.bass as bass
import concourse.mybir as mybir
from concourse.bass_utils import run_bass_kernel_spmd
F32 = mybir.dt.float32
BF16 = mybir.dt.bfloat16
I32 = mybir.dt.int32
AF = mybir.ActivationFunctionType
ALU = mybir.AluOpType
AX = mybir.AxisListType
D = 1024
KC = 8
NMEM = 256
DEPTH = 2
SEQ = 4096
NB = 8
EPS = 1e-06
TB = 512
NCOL = 88
ROPE_THETA = 500000.0
ENGS = ('pe', 'act', 'dve', 'pool', 'sp')
ATTACH_WAITS = True

class Buf:
    __slots__ = ('name', 'lw', 'rs')

    def __init__(self, name=''):
        self.name = name
        self.lw = None
        self.rs = []

class Op:
    __slots__ = ('eng', 'fn', 'dma', 'idx', 'deps', 'need', 'sem', 'val', 'pre')

    def __init__(self, eng, fn, dma, idx):
        self.eng = eng
        self.fn = fn
        self.dma = dma
        self.idx = idx
        self.deps = []
        self.need = False
        self.sem = None
        self.val = None
        self.pre = None

class Prog:

    def __init__(self, nc, n_dma_sems=24):
        self.nc = nc
        self.ops = []
        self.n_dma_sems = n_dma_sems
        self.since = []

    def op(self, eng, fn, reads=(), writes=(), dma=False):
        o = Op(eng, fn, dma, len(self.ops))
        deps = {}
        raw_src = set()
        for b in reads:
            if b.lw is not None:
                deps[b.lw.idx] = b.lw
                raw_src.add(b.lw.idx)
        for b in writes:
            if b.lw is not None:
                deps[b.lw.idx] = b.lw
            for r in b.rs:
                deps[r.idx] = r
        for b in reads:
            b.rs.append(o)
        for b in writes:
            b.lw = o
            b.rs = []
        for d in deps.values():
            if d is o:
                continue
            if not d.dma and (not dma) and (d.eng == eng):
                if eng == 'pe':
                    continue
                if d.idx not in raw_src:
                    continue
            o.deps.append(d)
            d.need = True
        self.ops.append(o)
        self.since.append(o)
        return o

    def dma(self, eng, fn, reads=(), writes=()):
        return self.op(eng, fn, reads, writes, dma=True)

    def barrier(self):
        last = {}
        dmas = []
        for o in self.since:
            if o.fn is None:
                continue
            if o.dma:
                dmas.append(o)
            else:
                last[o.eng] = o
        deps = list(last.values()) + dmas
        self.since = []
        for e in ENGS:
            w = Op(e, None, False, len(self.ops))
            for d in deps:
                if not d.dma and d.eng == e:
                    continue
                w.deps.append(d)
                d.need = True
            self.ops.append(w)

    def emit(self):
        nc = self.nc
        with ExitStack() as st:
            esem = {e: st.enter_context(nc.semaphore('s_' + e)) for e in ENGS}
            dsem = {e: [st.enter_context(nc.semaphore('d_%s_%d' % (e, i))) for i in range(self.n_dma_sems)] for e in ('sp', 'pool', 'act')}
            ecount = {e: 0 for e in ENGS}
            dcount = {e: [0] * self.n_dma_sems for e in dsem}
            drr = {e: 0 for e in dsem}
            for o in self.ops:
                if o.fn is None:
                    continue
                if o.dma:
                    k = drr[o.eng] % self.n_dma_sems
                    drr[o.eng] += 1
                    o.sem = dsem[o.eng][k]
                    o.pre = (o.sem, dcount[o.eng][k])
                    dcount[o.eng][k] += 16
                    o.val = dcount[o.eng][k]
                elif o.need:
                    ecount[o.eng] += 1
                    o.sem = esem[o.eng]
                    o.val = ecount[o.eng]
            per = {e: [o for o in self.ops if o.eng == e] for e in ENGS}
            self.stats = {e: len(per[e]) for e in ENGS}

            def run(e, eng):
                waited = {}
                for o in per[e]:
                    ws = [(d.sem, d.val) for d in o.deps if d.sem is not None]
                    if o.dma and o.pre[1] > 0:
                        ws.append(o.pre)
                    best = {}
                    for s, v in ws:
                        key = id(s)
                        if waited.get(key, 0) >= v:
                            continue
                        if key not in best or best[key][1] < v:
                            best[key] = (s, v)
                    items = list(best.values())
                    for s, v in items:
                        waited[id(s)] = v
                    attach = None
                    if o.fn is not None and items and ATTACH_WAITS:
                        attach = items.pop()
                    for s, v in items:
                        eng.wait_ge(s, v)
                    if o.fn is None:
                        continue
                    ins = getattr(eng, o.fn[0])(**o.fn[1])
                    if attach is not None:
                        ins._wait_ge(attach[0], attach[1])
                    if o.dma:
                        ins.then_inc(o.sem, 16)
                    elif o.need:
                        ins.then_inc(o.sem, 1)
            with nc.Block() as block:

                @block.sync
                def _(eng):
                    run('sp', eng)

                @block.tensor
                def _(eng):
                    run('pe', eng)

                @block.vector
                def _(eng):
                    run('dve', eng)

                @block.scalar
                def _(eng):
                    run('act', eng)

                @block.gpsimd
                def _(eng):
                    run('pool', eng)

class Ring:
    def __init__(self, tensors):
        self.t = tensors
        self.b = [Buf() for _ in tensors]
        self.i = 0

    def next(self):
        k = self.i % len(self.t)
        self.i += 1
        return (self.t[k], self.b[k])

def build(S=SEQ, depth=DEPTH, debug=False):
    NTB = S // TB
    SEG = S // 16
    NKB = S // 128
    nc = bass.Bass('TRN2', target_bir_lowering=False)
    P = Prog(nc)

    def dram_in(name, shape, dt=F32):
        return nc.dram_tensor(name, list(shape), dt, kind='ExternalInput').ap()
    skind = 'ExternalOutput' if debug else 'Internal'

    def dram_scr(name, shape, dt=F32):
        return nc.dram_tensor(name, list(shape), dt, kind=skind).ap()
    x_in = dram_in('x', [S, D])
    mem_in = dram_in('mem', [NMEM, D])
    pos_in = dram_in('pos', [1, S], I32)
    w_in_d = dram_in('w_in', [depth, D, 2560])
    w_out_d = dram_in('w_out', [depth, D, D])
    wq_d = dram_in('xa_wq', [depth, D, D])
    wk_d = dram_in('xa_wk', [depth, D, D])
    wv_d = dram_in('xa_wv', [depth, D, D])
    wo_d = dram_in('xa_wo', [depth, D, D])
    w1_d = dram_in('mlp_w1', [depth, D, 4 * D])
    w2_d = dram_in('mlp_w2', [depth, 4 * D, D])
    bd_d = dram_in('lru_bd', [depth, 128, 16 * 128])
    cols_d = dram_in('cols', [depth, 128, NCOL])
    lamv_d = dram_in('lamv', [depth, 1, 256])
    gsub_d = dram_in('gsub', [depth, 128, 1])
    ident_d = dram_in('ident', [128, 128])
    pmat_d = dram_in('pmat', [128, 128])
    fs_d = dram_in('fscol', [128, 3])
    out_d = nc.dram_tensor('out', [S, D], F32, kind='ExternalOutput').ap()
    xT_d = dram_scr('xT', [D, S])
    memT_d = dram_scr('memT', [D, NMEM])
    ctab_d = dram_scr('ctab', [128, S], BF16)
    stab_d = dram_scr('stab', [128, S], BF16)
    qk_d = dram_scr('qkT', [D, S], BF16)
    xb_d = dram_scr('xbT', [512, S])
    gg_d = dram_scr('ggT', [512, S])
    mix_d = dram_scr('mixT', [D, S], BF16)
    with ExitStack() as gst:

        uid = [0]

        def sb(st, name, shape, dt):
            uid[0] += 1
            return st.enter_context(nc.sbuf_tensor('sb%d_%s' % (uid[0], name), list(shape), dt))
        ident = sb(gst, 'ident', [128, 128], F32)
        identb = sb(gst, 'identb', [128, 128], BF16)
        pmat = sb(gst, 'pmat', [128, 128], BF16)
        ones = sb(gst, 'ones', [128, 128], BF16)
        fs = sb(gst, 'fs', [128, 3], F32)
        cc_ = sb(gst, 'cconst', [128, 4], F32)
        cols = sb(gst, 'cols', [128, depth * NCOL], F32)
        B_const = Buf('const')
        pairs = [gst.enter_context(nc.psum_tensor('pp%d' % i, [128, 2, 512], F32)) for i in range(4)]
        banks = [pairs[i // 2][:, i % 2, :] for i in range(8)]
        bbank = [Buf('bank%d' % i) for i in range(8)]
        P.dma('sp', ('dma_start', dict(out=ident[:], in_=ident_d[:, :])), writes=[B_const])
        P.dma('pool', ('dma_start', dict(out=identb[:], in_=ident_d[:, :])), writes=[B_const])
        P.dma('pool', ('dma_start', dict(out=pmat[:], in_=pmat_d[:, :])), writes=[B_const])
        P.dma('sp', ('dma_start', dict(out=fs[:], in_=fs_d[:, :])), writes=[B_const])
        for l in range(depth):
            P.dma('sp', ('dma_start', dict(out=cols[:, l * NCOL:(l + 1) * NCOL], in_=cols_d[l])), writes=[B_const])
        P.op('dve', ('memset', dict(ap=ones[:], constant=1.0)), writes=[B_const])
        P.op('dve', ('memset', dict(ap=cc_[:, 0:1], constant=EPS)), writes=[B_const])
        P.op('dve', ('memset', dict(ap=cc_[:, 1:2], constant=1.0)), writes=[B_const])
        P.op('dve', ('memset', dict(ap=cc_[:, 2:3], constant=0.0)), writes=[B_const])
        eps_c = cc_[:, 0:1]
        one_c = cc_[:, 1:2]
        P.barrier()

        def col(l, j):
            return cols[:, l * NCOL + j:l * NCOL + j + 1]

        def norm1(xin, bx, nch, N, sq, bsq):
            P.op('act', ('activation', dict(out=sq[:, 0:nch, 0:N], in_=xin[:, 0:nch, 0:N], func=AF.Square)), reads=[bx], writes=[bsq])

        def norm2(xin, bx, nch, N, sq, bsq, ssbank, bss, lnv, blnv, rstd, brstd, hT, bhT, gcol_fn, inv_n):
            for c in range(nch):
                P.op('pe', ('matmul', dict(out=ssbank[:, 0:N], lhsT=ones[:], rhs=sq[:, c, 0:N], start=c == 0, stop=c == nch - 1)), reads=[bsq, B_const], writes=[bss])
            P.op('act', ('activation', dict(out=lnv[:, 0:N], in_=ssbank[:, 0:N], func=AF.Ln, bias=eps_c, scale=inv_n)), reads=[bss], writes=[blnv])
            P.op('act', ('activation', dict(out=rstd[:, 0:N], in_=lnv[:, 0:N], func=AF.Exp, scale=-0.5)), reads=[blnv], writes=[brstd])
            for c in range(nch):
                P.op('dve', ('scalar_tensor_tensor', dict(out=hT[:, c, 0:N], in0=xin[:, c, 0:N], scalar=gcol_fn(c), in1=rstd[:, 0:N], op0=ALU.mult, op1=ALU.mult)), reads=[bx, brstd], writes=[bhT])

        def rmsnorm_fm(xin, bx, nch, N, sq, bsq, ssbank, bss, lnv, blnv, rstd, brstd, hT, bhT, gcol_fn, inv_n):
            norm1(xin, bx, nch, N, sq, bsq)
            norm2(xin, bx, nch, N, sq, bsq, ssbank, bss, lnv, blnv, rstd, brstd, hT, bhT, gcol_fn, inv_n)
        with ExitStack() as st:
            xtok = [sb(st, 'p_xtok%d' % i, [128, D], F32) for i in range(2)]
            r_xtok = Ring(xtok)
            stg = [sb(st, 'p_stg%d' % i, [128, KC, TB], F32) for i in range(2)]
            r_stg = Ring(stg)
            r_bank = Ring(banks[0:4])
            r_bank.b = bbank[0:4]
            for tb in range(NTB):
                sg, bsg = r_stg.next()
                toks = []
                for ts in range(4):
                    xt, bxt = r_xtok.next()
                    r0 = tb * TB + ts * 128
                    P.dma('sp', ('dma_start', dict(out=xt[:], in_=x_in[r0:r0 + 128, :])), writes=[bxt])
                    toks.append((xt, bxt))
                    for c in range(KC):
                        pass
                    for half in range(2):
                        bk, bbk = r_bank.next()
                        for j in range(4):
                            c = half * 4 + j
                            P.op('pe', ('transpose', dict(out=bk[:, j * 128:(j + 1) * 128], in_=xt[:, c * 128:(c + 1) * 128], identity=ident[:])), reads=[bxt, B_const], writes=[bbk])
                        eng = 'dve' if half == 0 else 'act'
                        if eng == 'dve':
                            P.op('dve', ('tensor_copy', dict(out=sg[:, half * 4:half * 4 + 4, ts * 128:(ts + 1) * 128], in_=bk[:].rearrange('p (c t) -> p c t', c=4))), reads=[bbk], writes=[bsg])
                        else:
                            P.op('act', ('activation', dict(out=sg[:, half * 4:half * 4 + 4, ts * 128:(ts + 1) * 128], in_=bk[:].rearrange('p (c t) -> p c t', c=4), func=AF.Copy)), reads=[bbk], writes=[bsg])
                P.dma('pool', ('dma_start', dict(out=xT_d.rearrange('(c p) t -> p c t', p=128)[:, :, tb * TB:(tb + 1) * TB], in_=sg[:])), reads=[bsg])
            sg, bsg = r_stg.next()
            for mb in range(2):
                xt, bxt = r_xtok.next()
                P.dma('sp', ('dma_start', dict(out=xt[:], in_=mem_in[mb * 128:(mb + 1) * 128, :])), writes=[bxt])
                for half in range(2):
                    bk, bbk = r_bank.next()
                    for j in range(4):
                        c = half * 4 + j
                        P.op('pe', ('transpose', dict(out=bk[:, j * 128:(j + 1) * 128], in_=xt[:, c * 128:(c + 1) * 128], identity=ident[:])), reads=[bxt, B_const], writes=[bbk])
                    P.op('dve', ('tensor_copy', dict(out=sg[:, half * 4:half * 4 + 4, mb * 128:(mb + 1) * 128], in_=bk[:].rearrange('p (c t) -> p c t', c=4))), reads=[bbk], writes=[bsg])
            P.dma('pool', ('dma_start', dict(out=memT_d.rearrange('(c p) t -> p c t', p=128), in_=sg[:, :, 0:NMEM])), reads=[bsg])
            posi = sb(st, 'p_posi', [128, SEG], I32)
            ang = sb(st, 'p_ang', [128, SEG], F32)
            t1 = sb(st, 'p_t1', [128, SEG], F32)
            t2 = sb(st, 'p_t2', [128, SEG], F32)
            ni = sb(st, 'p_ni', [128, SEG], I32)
            tabc = sb(st, 'p_tabc', [128, SEG], BF16)
            tabs = sb(st, 'p_tabs', [128, SEG], BF16)
            tabn = sb(st, 'p_tabn', [128, SEG], BF16)
            cfill = sb(st, 'p_cfill', [128, S], BF16)
            zfill = sb(st, 'p_zfill', [128, S], BF16)
            b_ctd = Buf('ctab_d')
            b_std = Buf('stab_d')
            b_fill = Buf('fill')
            bR = Buf('rope')
            TWO_PI = 2.0 * np.pi
            C1 = 6.28125
            C2 = TWO_PI - C1
            for j in range(8):
                P.dma('sp', ('dma_start', dict(out=posi[j * 16:(j + 1) * 16, :], in_=pos_in.rearrange('o (g t) -> (o g) t', t=SEG))), writes=[bR])
            P.op('pool', ('memset', dict(ap=cfill[:], constant=1.0)), writes=[b_fill])
            P.op('pool', ('memset', dict(ap=zfill[:], constant=0.0)), writes=[b_fill])
            P.dma('pool', ('dma_start', dict(out=ctab_d[:, :], in_=cfill[:])), reads=[b_fill], writes=[b_ctd])
            P.dma('pool', ('dma_start', dict(out=stab_d[:, :], in_=zfill[:])), reads=[b_fill], writes=[b_std])
            P.op('dve', ('tensor_copy', dict(out=t1[:], in_=posi[:])), reads=[bR], writes=[bR])
            P.op('dve', ('tensor_scalar', dict(out=ang[:], in0=t1[:], scalar1=fs[:, 2:3], scalar2=None, op0=ALU.mult)), reads=[bR, B_const], writes=[bR])

            def reduce_sin(src_shift, dst_bf, signcol):
                P.op('dve', ('tensor_scalar', dict(out=t1[:], in0=ang[:], scalar1=float(src_shift), scalar2=None, op0=ALU.add)), reads=[bR], writes=[bR])
                P.op('dve', ('tensor_scalar', dict(out=t2[:], in0=t1[:], scalar1=float(1.0 / TWO_PI), scalar2=None, op0=ALU.mult)), reads=[bR], writes=[bR])
                P.op('dve', ('tensor_copy', dict(out=ni[:], in_=t2[:])), reads=[bR], writes=[bR])
                P.op('dve', ('tensor_copy', dict(out=t2[:], in_=ni[:])), reads=[bR], writes=[bR])
                P.op('dve', ('scalar_tensor_tensor', dict(out=t1[:], in0=t2[:], scalar=float(-C1), in1=t1[:], op0=ALU.mult, op1=ALU.add)), reads=[bR], writes=[bR])
                P.op('dve', ('scalar_tensor_tensor', dict(out=t1[:], in0=t2[:], scalar=float(-C2), in1=t1[:], op0=ALU.mult, op1=ALU.add)), reads=[bR], writes=[bR])
                P.op('dve', ('tensor_scalar', dict(out=t2[:], in0=t1[:], scalar1=float(np.pi), scalar2=float(-TWO_PI), op0=ALU.is_gt, op1=ALU.mult)), reads=[bR], writes=[bR])
                P.op('dve', ('tensor_tensor', dict(out=t1[:], in0=t1[:], in1=t2[:], op=ALU.add)), reads=[bR], writes=[bR])
                P.op('dve', ('tensor_scalar', dict(out=t2[:], in0=t1[:], scalar1=float(-np.pi), scalar2=float(TWO_PI), op0=ALU.is_lt, op1=ALU.mult)), reads=[bR], writes=[bR])
                P.op('dve', ('tensor_tensor', dict(out=t1[:], in0=t1[:], in1=t2[:], op=ALU.add)), reads=[bR], writes=[bR])
                P.op('dve', ('tensor_scalar', dict(out=t1[:], in0=t1[:], scalar1=float(np.pi - 1e-06), scalar2=float(-np.pi + 1e-06), op0=ALU.min, op1=ALU.max)), reads=[bR], writes=[bR])
                P.op('act', ('activation', dict(out=t2[:], in_=t1[:], func=AF.Sin)), reads=[bR], writes=[bR])
                if signcol is None:
                    P.op('dve', ('tensor_copy', dict(out=dst_bf[:], in_=t2[:])), reads=[bR], writes=[bR])
                else:
                    P.op('dve', ('tensor_scalar', dict(out=dst_bf[:], in0=t2[:], scalar1=signcol, scalar2=None, op0=ALU.mult)), reads=[bR, B_const], writes=[bR])
            reduce_sin(np.pi / 2.0, tabc, None)
            reduce_sin(0.0, tabs, None)
            P.op('dve', ('tensor_scalar', dict(out=tabn[:], in0=tabs[:], scalar1=-1.0, scalar2=None, op0=ALU.mult)), reads=[bR], writes=[bR])
            for r0 in (0, 8, 64, 72):
                P.dma('sp', ('dma_start', dict(out=ctab_d[r0:r0 + 8, :].rearrange('j (g t) -> (j g) t', t=SEG), in_=tabc[:])), reads=[bR], writes=[b_ctd])
            for r0 in (0, 64):
                P.dma('sp', ('dma_start', dict(out=stab_d[r0:r0 + 8, :].rearrange('j (g t) -> (j g) t', t=SEG), in_=tabn[:])), reads=[bR], writes=[b_std])
            for r0 in (8, 72):
                P.dma('sp', ('dma_start', dict(out=stab_d[r0:r0 + 8, :].rearrange('j (g t) -> (j g) t', t=SEG), in_=tabs[:])), reads=[bR], writes=[b_std])
        P.barrier()
        for l in range(depth):
            lam_init = 0.8 - 0.6 * float(np.exp(-0.3 * l))
            with ExitStack() as lst:
                v_all = sb(lst, 'v_all', [128, NKB, 4, 129], BF16)
                b_vall = Buf('v_all')
                with ExitStack() as st:
                    win = sb(st, 'a_win', [128, KC, 2560], BF16)
                    b_win = [Buf() for _ in range(KC)]
                    ctab = sb(st, 'a_ctab', [128, S], BF16)
                    stab = sb(st, 'a_stab', [128, S], BF16)
                    b_tab = Buf()
                    xin_r = Ring([sb(st, 'a_xin%d' % i, [128, KC, TB], F32) for i in range(2)])
                    sq = sb(st, 'a_sq', [128, KC, TB], BF16)
                    b_sq = Buf()
                    hT_r = Ring([sb(st, 'a_hT%d' % i, [128, KC, TB], BF16) for i in range(2)])
                    lnv = sb(st, 'a_lnv', [128, TB], F32)
                    b_lnv = Buf()
                    rstd = sb(st, 'a_rstd', [128, TB], F32)
                    b_rstd = Buf()
                    tbf_r = Ring([sb(st, 'a_tbf%d' % i, [128, TB], BF16) for i in range(2)])
                    tm1_r = Ring([sb(st, 'a_tm1%d' % i, [128, TB], F32) for i in range(2)])
                    tm2_r = Ring([sb(st, 'a_tm2%d' % i, [128, TB], F32) for i in range(2)])
                    qk_r = Ring([sb(st, 'a_qk%d' % i, [128, 8, TB], BF16) for i in range(2)])
                    xg_r = Ring([sb(st, 'a_xg%d' % i, [128, 8, TB], F32) for i in range(1)])
                    main_r = Ring(banks[0:4])
                    main_r.b = bbank[0:4]
                    sw_r = Ring(banks[5:7])
                    sw_r.b = bbank[5:7]
                    ssb, b_ssb = (banks[4], bbank[4])
                    for kc in range(KC):
                        P.dma('pool', ('dma_start', dict(out=win[:, kc, :], in_=w_in_d[l, kc * 128:(kc + 1) * 128, :])), writes=[b_win[kc]])
                    P.dma('sp', ('dma_start', dict(out=ctab[:], in_=ctab_d[:, :])), writes=[b_tab])
                    P.dma('sp', ('dma_start', dict(out=stab[:], in_=stab_d[:, :])), writes=[b_tab])
                    P.op('dve', ('memset', dict(ap=v_all[:, :, :, 128:129], constant=1.0)), writes=[b_vall])

                    def load_x(tb):
                        xin, bxin = xin_r.next()
                        P.dma('sp', ('dma_start', dict(out=xin[:], in_=xT_d.rearrange('(c p) t -> p c t', p=128)[:, :, tb * TB:(tb + 1) * TB])), writes=[bxin])
                        return (xin, bxin)
                    sq_r = Ring([sq, sb(st, 'a_sq2', [128, KC, TB], BF16)])
                    cur = load_x(0)
                    hcur = hT_r.next()
                    sqc = sq_r.next()
                    norm1(cur[0], cur[1], KC, TB, sqc[0], sqc[1])
                    norm2(cur[0], cur[1], KC, TB, sqc[0], sqc[1], ssb, b_ssb, lnv, b_lnv, rstd, b_rstd, hcur[0], hcur[1], lambda c: col(l, 0 + c), 1.0 / D)
                    for tb in range(NTB):
                        hT, bhT = hcur
                        if tb + 1 < NTB:
                            nxt = load_x(tb + 1)
                            hnx = hT_r.next()
                            sqn = sq_r.next()
                            norm1(nxt[0], nxt[1], KC, TB, sqn[0], sqn[1])
                        qk, bqk = qk_r.next()
                        xg, bxg = xg_r.next()
                        tsl = slice(tb * TB, (tb + 1) * TB)
                        pend = None

                        def rope_tail(pd):
                            oc_, tbf, btbf = pd
                            tm1, btm1 = tm1_r.next()
                            tm2, btm2 = tm2_r.next()
                            sw, bsw = sw_r.next()
                            P.op('pe', ('matmul', dict(out=sw[:], lhsT=pmat[:], rhs=tbf[:], start=True, stop=True)), reads=[btbf, B_const], writes=[bsw])
                            P.op('dve', ('tensor_tensor', dict(out=tm1[:], in0=sw[:], in1=stab[:, tsl], op=ALU.mult)), reads=[bsw, b_tab], writes=[btm1])
                            P.op('pool', ('tensor_tensor', dict(out=tm2[:], in0=tbf[:], in1=ctab[:, tsl], op=ALU.mult)), reads=[btbf, b_tab], writes=[btm2])
                            P.op('dve', ('tensor_tensor', dict(out=qk[:, oc_, :], in0=tm1[:], in1=tm2[:], op=ALU.add)), reads=[btm1, btm2], writes=[bqk])
                        chunks = list(range(0, 8)) + list(range(12, 20))
                        for ci, oc in enumerate(chunks):
                            bk, bbk = main_r.next()
                            for kc in range(KC):
                                P.op('pe', ('matmul', dict(out=bk[:], lhsT=win[:, kc, oc * 128:(oc + 1) * 128], rhs=hT[:, kc, :], start=kc == 0, stop=kc == KC - 1)), reads=[b_win[kc], bhT], writes=[bbk])
                            if pend is not None:
                                rope_tail(pend)
                                pend = None
                            if oc < 8:
                                tbf, btbf = tbf_r.next()
                                P.op('act', ('activation', dict(out=tbf[:], in_=bk[:], func=AF.Copy)), reads=[bbk], writes=[btbf])
                                pend = (oc, tbf, btbf)
                            elif oc < 16:
                                P.op('dve', ('tensor_copy', dict(out=xg[:, oc - 12, :], in_=bk[:])), reads=[bbk], writes=[bxg])
                            else:
                                P.op('act', ('activation', dict(out=xg[:, oc - 12, :], in_=bk[:], func=AF.Gelu)), reads=[bbk], writes=[bxg])
                            if ci == 11 and tb + 1 < NTB:
                                norm2(nxt[0], nxt[1], KC, TB, sqn[0], sqn[1], ssb, b_ssb, lnv, b_lnv, rstd, b_rstd, hnx[0], hnx[1], lambda c: col(l, 0 + c), 1.0 / D)
                        for ts in range(4):
                            bk, bbk = main_r.next()
                            for kc in range(KC):
                                P.op('pe', ('matmul', dict(out=bk[:], lhsT=hT[:, kc, ts * 128:(ts + 1) * 128], rhs=win[:, kc, 1024:1536], start=kc == 0, stop=kc == KC - 1)), reads=[b_win[kc], bhT], writes=[bbk])
                            kb = tb * 4 + ts
                            P.op('act', ('activation', dict(out=v_all[:, kb, :, 0:128], in_=bk[:].rearrange('p (h e) -> p h e', h=4), func=AF.Copy)), reads=[bbk], writes=[b_vall])
                        if tb + 1 < NTB:
                            hcur = hnx
                        P.dma('pool', ('dma_start', dict(out=qk_d.rearrange('(c p) t -> p c t', p=128)[:, :, tb * TB:(tb + 1) * TB], in_=qk[:])), reads=[bqk])
                        P.dma('pool', ('dma_start', dict(out=xb_d.rearrange('(c p) t -> p c t', p=128)[:, :, tb * TB:(tb + 1) * TB], in_=xg[:, 0:4, :])), reads=[bxg])
                        P.dma('pool', ('dma_start', dict(out=gg_d.rearrange('(c p) t -> p c t', p=128)[:, :, tb * TB:(tb + 1) * TB], in_=xg[:, 4:8, :])), reads=[bxg])
                P.barrier()
                with ExitStack() as st:
                    qh_r = Ring([sb(st, 'b_qh%d' % i, [128, S], BF16) for i in range(2)])
                    kh_r = Ring([sb(st, 'b_kh%d' % i, [128, S], BF16) for i in range(2)])
                    pt_r = Ring([sb(st, 'b_pt%d' % i, [128, 2, TB], BF16) for i in range(5)])
                    ps2_r = Ring([sb(st, 'b_ps2%d' % i, [128, 2, TB], BF16) for i in range(3)])
                    lamv = sb(st, 'b_lamv', [128, 256], F32)
                    lamt = sb(st, 'b_lamt', [128, 128], F32)
                    lams = sb(st, 'b_lams', [128, 8], F32)
                    gsubc = sb(st, 'b_gsubc', [128, 1], F32)
                    b_lam = Buf()
                    o1 = sb(st, 'b_o1', [128, TB], F32)
                    b_o1 = Buf()
                    o2 = sb(st, 'b_o2', [128, TB], F32)
                    b_o2 = Buf()
                    oo_r = Ring([sb(st, 'b_oo%d' % i, [128, TB], F32) for i in range(2)])
                    osq_r = Ring([sb(st, 'b_osq%d' % i, [128, TB], BF16) for i in range(2)])
                    lnb = sb(st, 'b_lnb', [128, TB], F32)
                    b_lnb = Buf()
                    rsb = sb(st, 'b_rsb', [128, TB], F32)
                    b_rsb = Buf()
                    oT_r = Ring([sb(st, 'b_oT%d' % i, [128, TB], BF16) for i in range(2)])
                    s_r = Ring(pairs[0:2])
                    accp = pairs[2:4]
                    b_acc = [Buf(), Buf()]
                    P.dma('sp', ('dma_start', dict(out=lamv[:], in_=lamv_d[l].to_broadcast([128, 256]))), writes=[b_lam])
                    P.dma('sp', ('dma_start', dict(out=gsubc[:], in_=gsub_d[l])), writes=[b_lam])
                    P.op('dve', ('tensor_tensor', dict(out=lamt[:, 0:64], in0=lamv[:, 0:64], in1=lamv[:, 64:128], op=ALU.mult)), reads=[b_lam], writes=[b_lam])
                    P.op('dve', ('tensor_tensor', dict(out=lamt[:, 64:128], in0=lamv[:, 128:192], in1=lamv[:, 192:256], op=ALU.mult)), reads=[b_lam], writes=[b_lam])
                    P.op('dve', ('tensor_reduce', dict(out=lams[:, 0:2], in_=lamt[:].rearrange('p (a b) -> p a b', a=2), axis=AX.X, op=ALU.add)), reads=[b_lam], writes=[b_lam])
                    P.op('act', ('activation', dict(out=lams[:, 2:4], in_=lams[:, 0:2], func=AF.Exp)), reads=[b_lam], writes=[b_lam])
                    P.op('dve', ('tensor_tensor', dict(out=lams[:, 4:5], in0=lams[:, 3:4], in1=lams[:, 2:3], op=ALU.subtract)), reads=[b_lam], writes=[b_lam])
                    P.op('dve', ('tensor_scalar', dict(out=lams[:, 5:6], in0=lams[:, 4:5], scalar1=float(-lam_init), scalar2=None, op0=ALU.add)), reads=[b_lam], writes=[b_lam])
                    P.op('dve', ('tensor_scalar', dict(out=gsubc[:], in0=gsubc[:], scalar1=float(1.0 - lam_init), scalar2=None, op0=ALU.mult)), reads=[b_lam], writes=[b_lam])
                    nlam = lams[:, 5:6]

                    def load_head(h):
                        qh, bqh = qh_r.next()
                        kh, bkh = kh_r.next()
                        P.dma('sp', ('dma_start', dict(out=qh[:], in_=qk_d[h * 128:(h + 1) * 128, :])), writes=[bqh])
                        P.dma('sp', ('dma_start', dict(out=kh[:], in_=qk_d[512 + h * 128:512 + (h + 1) * 128, :])), writes=[bkh])
                        return (qh, bqh, kh, bkh)

                    sel = sb(st, 'b_sel', [64, 2, 128], F32)
                    dsb = sb(st, 'b_dsb', [64, TB], F32)
                    b_dsb = Buf()
                    rd64 = sb(st, 'b_rd64', [64, TB], F32)
                    b_rd64 = Buf()
                    oc2_r = Ring([sb(st, 'b_oc2%d' % i, [128, 2, TB], F32) for i in range(2)])
                    dbank = pairs[3][:, 0, :]
                    b_accd = Buf()
                    P.op('dve', ('memset', dict(ap=sel[:], constant=0.0)), writes=[b_lam])
                    P.op('dve', ('memset', dict(ap=sel[0:1, 0, :], constant=1.0)), writes=[b_lam])
                    P.op('dve', ('memset', dict(ap=sel[32:33, 1, :], constant=1.0)), writes=[b_lam])

                    def epiA(h, qb):
                        oc, boc = oc2_r.next()
                        P.op('dve', ('tensor_copy', dict(out=oc[:], in_=accp[0][:])), reads=[b_acc[0]], writes=[boc])
                        P.op('dve', ('tensor_copy', dict(out=dsb[:], in_=dbank[0:64, :])), reads=[b_accd], writes=[b_dsb])
                        P.op('dve', ('reciprocal', dict(out=rd64[:], in_=dsb[:])), reads=[b_dsb], writes=[b_rd64])
                        return (h, qb, oc, boc)

                    def epiB(h, qb, oc, boc):
                        sp, bsp = s_r.next()
                        for c in range(2):
                            P.op('pe', ('matmul', dict(out=sp[:, c, :], lhsT=sel[:, c, :], rhs=rd64[:], start=True, stop=True)), reads=[b_rd64, b_lam], writes=[bsp])
                        P.op('dve', ('tensor_tensor', dict(out=o1[:], in0=oc[:, 0, :], in1=sp[:, 0, :], op=ALU.mult)), reads=[boc, bsp], writes=[b_o1])
                        P.op('dve', ('tensor_tensor', dict(out=o2[:], in0=oc[:, 1, :], in1=sp[:, 1, :], op=ALU.mult)), reads=[boc, bsp], writes=[b_o2])
                        oo, boo = oo_r.next()
                        P.op('dve', ('scalar_tensor_tensor', dict(out=oo[:], in0=o2[:], scalar=nlam, in1=o1[:], op0=ALU.mult, op1=ALU.add)), reads=[b_o1, b_o2, b_lam], writes=[boo])
                        osq, bosq = osq_r.next()
                        P.op('pool', ('tensor_tensor', dict(out=osq[:], in0=oo[:], in1=oo[:], op=ALU.mult)), reads=[boo], writes=[bosq])
                        return (h, qb, oo, boo, osq, bosq)

                    def epi2(h, qb, oo, boo, osq, bosq):
                        sp, bsp = s_r.next()
                        P.op('pe', ('matmul', dict(out=sp[:, 0, :], lhsT=ones[:], rhs=osq[:], start=True, stop=True)), reads=[bosq, B_const], writes=[bsp])
                        P.op('act', ('activation', dict(out=lnb[:], in_=sp[:, 0, :], func=AF.Ln, bias=eps_c, scale=1.0 / 128.0)), reads=[bsp], writes=[b_lnb])
                        P.op('act', ('activation', dict(out=rsb[:], in_=lnb[:], func=AF.Exp, scale=-0.5)), reads=[b_lnb], writes=[b_rsb])
                        oT, boT = oT_r.next()
                        P.op('dve', ('scalar_tensor_tensor', dict(out=oT[:], in0=oo[:], scalar=gsubc[:, 0:1], in1=rsb[:], op0=ALU.mult, op1=ALU.mult)), reads=[boo, b_rsb, b_lam], writes=[boT])
                        P.dma('pool', ('dma_start', dict(out=mix_d[h * 128:(h + 1) * 128, qb * TB:(qb + 1) * TB], in_=oT[:])), reads=[boT])

                    def qk(qh, bqh, kh, bkh, qb, kb):
                        sp, bsp = s_r.next()
                        for c in range(2):
                            P.op('pe', ('matmul', dict(out=sp[:, c, :], lhsT=kh[c * 64:(c + 1) * 64, kb * 128:(kb + 1) * 128], rhs=qh[c * 64:(c + 1) * 64, qb * TB:(qb + 1) * TB], start=True, stop=True)), reads=[bqh, bkh], writes=[bsp])
                        pt, bpt = pt_r.next()
                        P.op('act', ('activation', dict(out=pt[:], in_=sp[:], func=AF.Exp, scale=0.125)), reads=[bsp], writes=[bpt])
                        return (pt, bpt)

                    def pv(h, kb, pt, bpt):
                        for c in range(2):
                            P.op('pe', ('matmul', dict(out=accp[0][:, c, :], lhsT=v_all[:, kb, h, 0:128], rhs=pt[:, c, :], start=kb == 0, stop=kb == NKB - 1)), reads=[bpt, b_vall], writes=[b_acc[0]])

                    def den(ps2, bps2, pair):
                        for c in range(2):
                            P.op('pe', ('matmul', dict(out=dbank[c * 32:(c + 1) * 32, :], lhsT=ones[:, 0:32], rhs=ps2[:, c, :], start=pair == 0, stop=pair == NKB // 2 - 1)), reads=[bps2, B_const], writes=[b_accd])
                    pendB = None
                    pendC = None
                    tB = NKB // 4
                    tC = 3 * NKB // 4
                    nxth = load_head(0)
                    for h in range(4):
                        qh, bqh, kh, bkh = nxth
                        if h + 1 < 4:
                            nxth = load_head(h + 1)
                        for qb in range(NTB):
                            qq = [qk(qh, bqh, kh, bkh, qb, 0), qk(qh, bqh, kh, bkh, qb, 1)]
                            prev = None
                            pden = None
                            for kb in range(NKB):
                                if kb + 2 < NKB:
                                    qq.append(qk(qh, bqh, kh, bkh, qb, kb + 2))
                                cur = qq[kb]
                                nx = None
                                pv(h, kb, *cur)
                                if pden is not None:
                                    den(*pden)
                                    pden = None
                                if kb % 2 == 1:
                                    ps2, bps2 = ps2_r.next()
                                    P.op('dve', ('tensor_tensor', dict(out=ps2[:], in0=prev[0][:], in1=cur[0][:], op=ALU.add)), reads=[prev[1], cur[1]], writes=[bps2])
                                    pden = (ps2, bps2, kb // 2)
                                prev = cur
                                if kb == tB and pendB is not None:
                                    pendC = epiB(*pendB)
                                    pendB = None
                                if kb == tC and pendC is not None:
                                    epi2(*pendC)
                                    pendC = None
                            den(*pden)
                            pendB = epiA(h, qb)
                    pendC = epiB(*pendB)
                    epi2(*pendC)
                P.barrier()
            with ExitStack() as st:
                bd = sb(st, 'c_bd', [128, 16, 128], BF16)
                b_bd = Buf()
                cbf = sb(st, 'c_cbf', [128, 4, S], BF16)
                b_cbf = [Buf() for _ in range(NTB)]
                hf = sb(st, 'c_hf', [128, 4, S], F32)
                b_hf = [Buf() for _ in range(NTB)]
                xb_r = Ring([sb(st, 'c_xb%d' % i, [128, 4, TB + 3], F32) for i in range(2)])
                dg = sb(st, 'c_dg', [128, 16, 128], F32)
                b_dg = Buf()
                gr = [sb(st, 'c_r%d' % i, [128, TB], F32) for i in range(4)]
                gi = [sb(st, 'c_i%d' % i, [128, TB], F32) for i in range(4)]
                ga = [sb(st, 'c_a%d' % i, [128, TB], F32) for i in range(4)]
                gt = gr
                gu = gi
                b_gr = [Buf() for _ in range(4)]
                b_gi = [Buf() for _ in range(4)]
                b_ga = [Buf() for _ in range(4)]
                b_gt = b_gr
                b_gu = b_gi
                hb_r = Ring([sb(st, 'c_hb%d' % i, [128, 4, TB], F32) for i in range(2)])
                gg_r = Ring([sb(st, 'c_gg%d' % i, [128, 4, TB], F32) for i in range(2)])
                yy = sb(st, 'c_y', [128, 4, TB], F32)
                b_yy = Buf()
                ysq = sb(st, 'c_ysq', [128, 4, TB], BF16)
                b_ysq = Buf()
                lnv = sb(st, 'c_lnv', [128, TB], F32)
                b_lnv = Buf()
                rstd = sb(st, 'c_rstd', [128, TB], F32)
                b_rstd = Buf()
                rT_r = Ring([sb(st, 'c_rT%d' % i, [128, 4, TB], BF16) for i in range(1)])
                nsp = sb(st, 'c_nsp', [128, 16], F32)
                b_nsp = Buf()
                g_r = Ring(banks[0:6])
                g_r.b = bbank[0:6]
                ssb, b_ssb = (banks[6], bbank[6])
                P.dma('pool', ('dma_start', dict(out=bd[:], in_=bd_d[l].rearrange('k (j m) -> k j m', j=16))), writes=[b_bd])
                P.op('act', ('activation', dict(out=nsp[:, 0:8], in_=cols[:, l * NCOL + 68:l * NCOL + 76], func=AF.Exp, scale=-1.0)), reads=[B_const], writes=[b_nsp])
                P.op('act', ('activation', dict(out=nsp[:, 0:8], in_=nsp[:, 0:8], func=AF.Ln, bias=one_c, scale=1.0)), reads=[b_nsp], writes=[b_nsp])
                P.op('dve', ('tensor_scalar', dict(out=nsp[:, 8:16], in0=nsp[:, 0:8], scalar1=-16.0, scalar2=None, op0=ALU.mult)), reads=[b_nsp], writes=[b_nsp])
                P.op('dve', ('tensor_scalar', dict(out=nsp[:, 0:8], in0=nsp[:, 0:8], scalar1=-8.0, scalar2=None, op0=ALU.mult)), reads=[b_nsp], writes=[b_nsp])

                def gates(dr, tb, out_fn, bout, init_fn, rev):
                    tsl = slice(tb * TB, (tb + 1) * TB)
                    for cc in range(4):
                        for g, (dst, bdst, cbase) in enumerate(((gr, b_gr, 52), (gi, b_gi, 60))):
                            bk, bbk = g_r.next()
                            P.op('pe', ('matmul', dict(out=bk[:], lhsT=bd[:, dr * 8 + g * 4 + cc, :], rhs=cbf[:, cc, tsl], start=True, stop=True)), reads=[b_bd, b_cbf[tb]], writes=[bbk])
                            P.op('act', ('activation', dict(out=dst[cc][:], in_=bk[:], func=AF.Sigmoid, bias=col(l, cbase + dr * 4 + cc))), reads=[bbk, B_const], writes=[bdst[cc]])
                    for cc in range(4):
                        P.op('act', ('activation', dict(out=ga[cc][:], in_=gr[cc][:], func=AF.Exp, scale=nsp[:, dr * 4 + cc:dr * 4 + cc + 1])), reads=[b_gr[cc], b_nsp], writes=[b_ga[cc]])
                        P.op('act', ('activation', dict(out=gt[cc][:], in_=gr[cc][:], func=AF.Exp, scale=nsp[:, 8 + dr * 4 + cc:8 + dr * 4 + cc + 1])), reads=[b_gr[cc], b_nsp], writes=[b_gt[cc]])
                        P.op('act', ('activation', dict(out=gt[cc][:], in_=gt[cc][:], func=AF.Ln, bias=one_c, scale=-1.0)), reads=[b_gt[cc]], writes=[b_gt[cc]])
                        P.op('act', ('activation', dict(out=gt[cc][:], in_=gt[cc][:], func=AF.Exp, scale=0.5)), reads=[b_gt[cc]], writes=[b_gt[cc]])
                        P.op('pool', ('tensor_tensor', dict(out=gu[cc][:], in0=cbf[:, cc, tsl], in1=gi[cc][:], op=ALU.mult)), reads=[b_cbf[tb], b_gi[cc]], writes=[b_gu[cc]])
                        P.op('dve', ('tensor_tensor', dict(out=gu[cc][:], in0=gu[cc][:], in1=gt[cc][:], op=ALU.mult)), reads=[b_gu[cc], b_gt[cc]], writes=[b_gu[cc]])
                        o_ap = out_fn(cc)
                        ini, bini = init_fn(cc)
                        if not rev:
                            P.op('dve', ('tensor_tensor_scan', dict(out=o_ap, data0=ga[cc][:], data1=gu[cc][:], initial=ini, op0=ALU.mult, op1=ALU.add)), reads=[b_ga[cc], b_gu[cc]] + bini, writes=[bout])
                        else:
                            P.op('dve', ('tensor_tensor_scan', dict(out=o_ap[:, ::-1], data0=ga[cc][:, ::-1], data1=gu[cc][:, ::-1], initial=ini, op0=ALU.mult, op1=ALU.add)), reads=[b_ga[cc], b_gu[cc]] + bini, writes=[bout])

                def load_xb(tb):
                    xb, bxb = xb_r.next()
                    lo = tb * TB - 2
                    hi = tb * TB + TB + 1
                    slo = max(lo, 0)
                    shi = min(hi, S)
                    if lo < 0:
                        P.op('pool', ('memset', dict(ap=xb[:, :, 0:2], constant=0.0)), writes=[bxb])
                    if hi > S:
                        P.op('pool', ('memset', dict(ap=xb[:, :, TB + 2:TB + 3], constant=0.0)), writes=[bxb])
                    P.dma('sp', ('dma_start', dict(out=xb[:, :, slo - lo:slo - lo + (shi - slo)], in_=xb_d.rearrange('(c p) t -> p c t', p=128)[:, :, slo:shi])), writes=[bxb])
                    return (xb, bxb)
                for jc in range(16):
                    P.op('dve', ('tensor_scalar', dict(out=dg[:, jc, :], in0=ident[:], scalar1=col(l, 32 + jc), scalar2=None, op0=ALU.mult)), reads=[B_const], writes=[b_dg])

                def conv(tb, xb, bxb):
                    tsl = slice(tb * TB, (tb + 1) * TB)
                    for cc in range(4):
                        bk, bbk = g_r.next()
                        for j in range(4):
                            P.op('pe', ('matmul', dict(out=bk[:], lhsT=dg[:, j * 4 + cc, :], rhs=xb[:, cc, j:j + TB], start=j == 0, stop=j == 3)), reads=[b_dg, bxb], writes=[bbk])
                        P.op('dve', ('tensor_scalar', dict(out=cbf[:, cc, tsl], in0=bk[:], scalar1=col(l, 48 + cc), scalar2=None, op0=ALU.add)), reads=[bbk, B_const], writes=[b_cbf[tb]])
                conv(0, *load_xb(0))
                for tb in range(NTB):
                    if tb + 1 < NTB:
                        conv(tb + 1, *load_xb(tb + 1))
                    tsl = slice(tb * TB, (tb + 1) * TB)

                    def init_f(cc, tb=tb):
                        if tb == 0:
                            return (0.0, [])
                        return (hf[:, cc, tb * TB - 1:tb * TB], [b_hf[tb - 1]])
                    gates(0, tb, lambda cc, tsl=tsl: hf[:, cc, tsl], b_hf[tb], init_f, False)

                def tail(tb, hb, bhb, gg, bgg):
                    tsl = slice(tb * TB, (tb + 1) * TB)
                    P.op('dve', ('tensor_tensor', dict(out=yy[:], in0=hf[:, :, tsl], in1=hb[:], op=ALU.add)), reads=[b_hf[tb], bhb], writes=[b_yy])
                    P.op('pool', ('tensor_tensor', dict(out=yy[:], in0=yy[:], in1=gg[:], op=ALU.mult)), reads=[b_yy, bgg], writes=[b_yy])
                    rT, brT = rT_r.next()
                    rmsnorm_fm(yy, b_yy, 4, TB, ysq, b_ysq, ssb, b_ssb, lnv, b_lnv, rstd, b_rstd, rT, brT, lambda c: col(l, 76 + c), 1.0 / 512.0)
                    P.dma('pool', ('dma_start', dict(out=mix_d.rearrange('(c p) t -> p c t', p=128)[:, 4:8, tb * TB:(tb + 1) * TB], in_=rT[:])), reads=[brT])
                prev_hb = None
                pend = None
                for tb in range(NTB - 1, -1, -1):
                    hb, bhb = hb_r.next()
                    gg, bgg = gg_r.next()
                    P.dma('sp', ('dma_start', dict(out=gg[:], in_=gg_d.rearrange('(c p) t -> p c t', p=128)[:, :, tb * TB:(tb + 1) * TB])), writes=[bgg])

                    def init_b(cc, prev_hb=prev_hb):
                        if prev_hb is None:
                            return (0.0, [])
                        return (prev_hb[0][:, cc, 0:1], [prev_hb[1]])
                    gates(1, tb, lambda cc, hb=hb: hb[:, cc, :], bhb, init_b, True)
                    prev_hb = (hb, bhb)
                    if pend is not None:
                        tail(*pend)
                    pend = (tb, hb, bhb, gg, bgg)
                tail(*pend)
            P.barrier()
            with ExitStack() as st:
                wout = sb(st, 'd_wout', [128, KC, D], BF16)
                pass
                pass
                b_wout, b_wq, b_wo = (Buf(), Buf(), Buf())
                kxT = sb(st, 'd_kxT', [128, KC, NMEM], BF16)
                vx = sb(st, 'd_vx', [128, 2, D], BF16)
                b_kx, b_vx = (Buf(), Buf())
                mix_r = Ring([sb(st, 'd_mix%d' % i, [128, KC, TB], BF16) for i in range(2)])
                xin_r = Ring([sb(st, 'd_xin%d' % i, [128, KC, TB], F32) for i in range(2)])
                pass
                sq = sb(st, 'd_sq', [128, KC, TB], BF16)
                b_sq = Buf()
                qx = sb(st, 'd_qx', [128, KC, TB], BF16)
                b_qx = [Buf() for _ in range(KC)]
                ox = sb(st, 'd_ox', [128, KC, TB], BF16)
                b_ox = [Buf() for _ in range(KC)]
                lnv = sb(st, 'd_lnv', [128, TB], F32)
                b_lnv = Buf()
                rstd = sb(st, 'd_rstd', [128, TB], F32)
                b_rstd = Buf()
                pt_r = Ring([sb(st, 'd_pt%d' % i, [128, 2, TB], BF16) for i in range(2)])
                lden = sb(st, 'd_lden', [128, TB], F32)
                b_lden = Buf()
                rden_r = Ring([sb(st, 'd_rden%d' % i, [128, TB], F32) for i in range(2)])
                main_r = Ring(banks[0:5])
                main_r.b = bbank[0:5]
                ssb, b_ssb = (banks[5], bbank[5])
                denb, b_denb = (banks[6], bbank[6])
                wq = sb(st, 'd_wq', [128, KC, D], BF16)
                wo = sb(st, 'd_wo', [128, KC, D], BF16)
                with ExitStack() as st2:
                    wk = sb(st2, 'd_wk', [128, KC, D], BF16)
                    wv = sb(st2, 'd_wv', [128, KC, D], BF16)
                    b_wk, b_wv = (Buf(), Buf())
                    memT = sb(st2, 'd_memT', [128, KC, NMEM], F32)
                    b_memT = Buf()
                    mT = sb(st2, 'd_mT', [128, KC, NMEM], BF16)
                    b_mT = Buf()
                    P.dma('sp', ('dma_start', dict(out=memT[:], in_=memT_d.rearrange('(c p) t -> p c t', p=128))), writes=[b_memT])
                    P.dma('pool', ('dma_start', dict(out=wk[:], in_=wk_d[l].rearrange('(c p) n -> p c n', p=128))), writes=[b_wk])
                    P.dma('pool', ('dma_start', dict(out=wv[:], in_=wv_d[l].rearrange('(c p) n -> p c n', p=128))), writes=[b_wv])
                    P.dma('pool', ('dma_start', dict(out=wout[:], in_=w_out_d[l].rearrange('(c p) n -> p c n', p=128))), writes=[b_wout])
                    P.dma('pool', ('dma_start', dict(out=wq[:], in_=wq_d[l].rearrange('(c p) n -> p c n', p=128))), writes=[b_wq])
                    P.dma('pool', ('dma_start', dict(out=wo[:], in_=wo_d[l].rearrange('(c p) n -> p c n', p=128))), writes=[b_wo])
                    pass
                    pass
                    rmsnorm_fm(memT, b_memT, KC, NMEM, sq, b_sq, ssb, b_ssb, lnv, b_lnv, rstd, b_rstd, mT, b_mT, lambda c: col(l, 16 + c), 1.0 / D)
                    for oc in range(KC):
                        bk, bbk = main_r.next()
                        for kc in range(KC):
                            P.op('pe', ('matmul', dict(out=bk[:, 0:NMEM], lhsT=wk[:, kc, oc * 128:(oc + 1) * 128], rhs=mT[:, kc, :], start=kc == 0, stop=kc == KC - 1)), reads=[b_wk, b_mT], writes=[bbk])
                        P.op('act', ('activation', dict(out=kxT[:, oc, :], in_=bk[:, 0:NMEM], func=AF.Copy)), reads=[bbk], writes=[b_kx])
                    for mb in range(2):
                        for half in range(2):
                            bk, bbk = main_r.next()
                            for kc in range(KC):
                                P.op('pe', ('matmul', dict(out=bk[:], lhsT=mT[:, kc, mb * 128:(mb + 1) * 128], rhs=wv[:, kc, half * 512:(half + 1) * 512], start=kc == 0, stop=kc == KC - 1)), reads=[b_wv, b_mT], writes=[bbk])
                            P.op('dve', ('tensor_copy', dict(out=vx[:, mb, half * 512:(half + 1) * 512], in_=bk[:])), reads=[bbk], writes=[b_vx])
                    P.barrier()

                def load_de(tb):
                    mix, bmix = mix_r.next()
                    xin, bxin = xin_r.next()
                    P.dma('sp', ('dma_start', dict(out=mix[:], in_=mix_d.rearrange('(c p) t -> p c t', p=128)[:, :, tb * TB:(tb + 1) * TB])), writes=[bmix])
                    P.dma('sp', ('dma_start', dict(out=xin[:], in_=xT_d.rearrange('(c p) t -> p c t', p=128)[:, :, tb * TB:(tb + 1) * TB])), writes=[bxin])
                    return (mix, bmix, xin, bxin)
                x1_r = Ring([sb(st, 'd_x1%d' % i, [128, KC, TB], F32) for i in range(2)])
                hq_r = Ring([sb(st, 'd_hq%d' % i, [128, KC, TB], BF16) for i in range(2)])

                def phase1(mix, bmix, xin, bxin):
                    x1, bx1 = x1_r.next()
                    for oc in range(KC):
                        bk, bbk = main_r.next()
                        for kc in range(KC):
                            P.op('pe', ('matmul', dict(out=bk[:], lhsT=wout[:, kc, oc * 128:(oc + 1) * 128], rhs=mix[:, kc, :], start=kc == 0, stop=kc == KC - 1)), reads=[b_wout, bmix], writes=[bbk])
                        P.op('dve', ('tensor_tensor', dict(out=x1[:, oc, :], in0=bk[:], in1=xin[:, oc, :], op=ALU.add)), reads=[bbk, bxin], writes=[bx1])
                    norm1(x1, bx1, KC, TB, sq, b_sq)
                    return (x1, bx1)

                def phase1b(x1, bx1):
                    hq, b_hq = hq_r.next()
                    norm2(x1, bx1, KC, TB, sq, b_sq, ssb, b_ssb, lnv, b_lnv, rstd, b_rstd, hq, b_hq, lambda c: col(l, 8 + c), 1.0 / D)
                    return (hq, b_hq)

                def qproj(hq, b_hq):
                    for oc in range(KC):
                        bk, bbk = main_r.next()
                        for kc in range(KC):
                            P.op('pe', ('matmul', dict(out=bk[:], lhsT=wq[:, kc, oc * 128:(oc + 1) * 128], rhs=hq[:, kc, :], start=kc == 0, stop=kc == KC - 1)), reads=[b_wq, b_hq], writes=[bbk])
                        P.op('act', ('activation', dict(out=qx[:, oc, :], in_=bk[:], func=AF.Copy)), reads=[bbk], writes=[b_qx[oc]])

                def scores(hh):
                    pt, bpt = pt_r.next()
                    for mb in range(2):
                        bk, bbk = main_r.next()
                        for dc in range(2):
                            P.op('pe', ('matmul', dict(out=bk[:], lhsT=kxT[:, 2 * hh + dc, mb * 128:(mb + 1) * 128], rhs=qx[:, 2 * hh + dc, :], start=dc == 0, stop=dc == 1)), reads=[b_kx, b_qx[2 * hh + dc]], writes=[bbk])
                        P.op('act', ('activation', dict(out=pt[:, mb, :], in_=bk[:], func=AF.Exp, scale=1.0 / 16.0)), reads=[bbk], writes=[bpt])
                    return (pt, bpt)

                def attn_tail(hh, pt, bpt):
                    for mb in range(2):
                        P.op('pe', ('matmul', dict(out=denb[:], lhsT=ones[:], rhs=pt[:, mb, :], start=mb == 0, stop=mb == 1)), reads=[bpt, B_const], writes=[b_denb])
                    rden, brden = rden_r.next()
                    P.op('act', ('activation', dict(out=lden[:], in_=denb[:], func=AF.Ln)), reads=[b_denb], writes=[b_lden])
                    P.op('act', ('activation', dict(out=rden[:], in_=lden[:], func=AF.Exp, scale=-1.0)), reads=[b_lden], writes=[brden])
                    for dc in range(2):
                        bk, bbk = main_r.next()
                        for mb in range(2):
                            P.op('pe', ('matmul', dict(out=bk[:], lhsT=vx[:, mb, (2 * hh + dc) * 128:(2 * hh + dc + 1) * 128], rhs=pt[:, mb, :], start=mb == 0, stop=mb == 1)), reads=[b_vx, bpt], writes=[bbk])
                        P.op('dve', ('tensor_tensor', dict(out=ox[:, 2 * hh + dc, :], in0=bk[:], in1=rden[:], op=ALU.mult)), reads=[bbk, brden], writes=[b_ox[2 * hh + dc]])

                def wo_phase(x1, bx1, tb):
                    for oc in range(KC):
                        bk, bbk = main_r.next()
                        for kc in range(KC):
                            P.op('pe', ('matmul', dict(out=bk[:], lhsT=wo[:, kc, oc * 128:(oc + 1) * 128], rhs=ox[:, kc, :], start=kc == 0, stop=kc == KC - 1)), reads=[b_wo, b_ox[kc]], writes=[bbk])
                        P.op('dve', ('tensor_tensor', dict(out=x1[:, oc, :], in0=bk[:], in1=x1[:, oc, :], op=ALU.add)), reads=[bbk, bx1], writes=[bx1])
                    P.dma('pool', ('dma_start', dict(out=xT_d.rearrange('(c p) t -> p c t', p=128)[:, :, tb * TB:(tb + 1) * TB], in_=x1[:])), reads=[bx1])
                ld = load_de(0)
                xcur = phase1(*ld)
                hcur = phase1b(*xcur)
                for tb in range(NTB):
                    more = tb + 1 < NTB
                    if more:
                        ldn = load_de(tb + 1)
                    qproj(*hcur)
                    if more:
                        xnx = phase1(*ldn)
                    ptc = scores(0)
                    for hh in range(4):
                        ptn = scores(hh + 1) if hh < 3 else None
                        attn_tail(hh, *ptc)
                        if hh == 1 and more:
                            hnx = phase1b(*xnx)
                        ptc = ptn
                    wo_phase(xcur[0], xcur[1], tb)
                    if more:
                        xcur = xnx
                        hcur = hnx
            P.barrier()
            with ExitStack() as st:
                w1 = sb(st, 'f_w1', [128, KC, 4 * D], BF16)
                w2 = sb(st, 'f_w2', [128, 32, D], BF16)
                b_w1 = [Buf() for _ in range(4)]
                b_w2 = [Buf() for _ in range(4)]
                xin_r = Ring([sb(st, 'f_xin%d' % i, [128, KC, TB], F32) for i in range(2)])
                pass
                b_sq = Buf()
                hm_r = Ring([sb(st, 'f_hm%d' % i, [128, KC, TB], BF16) for i in range(2)])
                act_r = Ring([sb(st, 'f_act%d' % i, [128, 8, TB], BF16) for i in range(2)])
                rl_r = Ring([sb(st, 'f_rl%d' % i, [128, TB], F32) for i in range(2)])
                lnv = sb(st, 'f_lnv', [128, TB], F32)
                b_lnv = Buf()
                rstd = sb(st, 'f_rstd', [128, TB], F32)
                b_rstd = Buf()
                main_r = Ring(banks[0:6])
                main_r.b = bbank[0:6]
                ssb, b_ssb = (banks[6], bbank[6])
                for g in range(4):
                    P.dma('pool', ('dma_start', dict(out=w1[:, :, g * 1024:(g + 1) * 1024], in_=w1_d[l].rearrange('(c p) n -> p c n', p=128)[:, :, g * 1024:(g + 1) * 1024])), writes=[b_w1[g]])
                    P.dma('pool', ('dma_start', dict(out=w2[:, g * 8:(g + 1) * 8, :], in_=w2_d[l].rearrange('(c p) n -> p c n', p=128)[:, g * 8:(g + 1) * 8, :])), writes=[b_w2[g]])

                def load_f(tb):
                    xin, bxin = xin_r.next()
                    P.dma('sp', ('dma_start', dict(out=xin[:], in_=xT_d.rearrange('(c p) t -> p c t', p=128)[:, :, tb * TB:(tb + 1) * TB])), writes=[bxin])
                    return (xin, bxin)
                cur = load_f(0)
                hcur = hm_r.next()
                sqc = act_r.next()
                norm1(cur[0], cur[1], KC, TB, sqc[0], sqc[1])
                norm2(cur[0], cur[1], KC, TB, sqc[0], sqc[1], ssb, b_ssb, lnv, b_lnv, rstd, b_rstd, hcur[0], hcur[1], lambda c: col(l, 24 + c), 1.0 / D)
                for tb in range(NTB):
                    xin, bxin = cur
                    hm, b_hm = hcur
                    if tb + 1 < NTB:
                        nxt = load_f(tb + 1)
                        hnx = hm_r.next()
                    for g in range(4):
                        at, bat = act_r.next()
                        for j in range(8):
                            fc = g * 8 + j
                            bk, bbk = main_r.next()
                            rl, brl = rl_r.next()
                            for kc in range(KC):
                                P.op('pe', ('matmul', dict(out=bk[:], lhsT=w1[:, kc, fc * 128:(fc + 1) * 128], rhs=hm[:, kc, :], start=kc == 0, stop=kc == KC - 1)), reads=[b_w1[g], b_hm], writes=[bbk])
                            P.op('act', ('activation', dict(out=rl[:], in_=bk[:], func=AF.Relu)), reads=[bbk], writes=[brl])
                            P.op('dve', ('tensor_tensor', dict(out=at[:, j, :], in0=bk[:], in1=rl[:], op=ALU.mult)), reads=[bbk, brl], writes=[bat])
                        if g == 1 and tb + 1 < NTB:
                            sqn = act_r.next()
                            norm1(nxt[0], nxt[1], KC, TB, sqn[0], sqn[1])
                        for oc in range(KC):
                            bk, bbk = main_r.next()
                            for j in range(8):
                                P.op('pe', ('matmul', dict(out=bk[:], lhsT=w2[:, g * 8 + j, oc * 128:(oc + 1) * 128], rhs=at[:, j, :], start=j == 0, stop=j == 7)), reads=[b_w2[g], bat], writes=[bbk])
                            P.op('dve', ('tensor_tensor', dict(out=xin[:, oc, :], in0=bk[:], in1=xin[:, oc, :], op=ALU.add)), reads=[bbk, bxin], writes=[bxin])
                        if g == 2 and tb + 1 < NTB:
                            norm2(nxt[0], nxt[1], KC, TB, sqn[0], sqn[1], ssb, b_ssb, lnv, b_lnv, rstd, b_rstd, hnx[0], hnx[1], lambda c: col(l, 24 + c), 1.0 / D)
                    if tb + 1 < NTB:
                        cur = nxt
                        hcur = hnx
                    P.dma('pool', ('dma_start', dict(out=xT_d.rearrange('(c p) t -> p c t', p=128)[:, :, tb * TB:(tb + 1) * TB], in_=xin[:])), reads=[bxin])
            P.barrier()
        with ExitStack() as st:
            xin_r = Ring([sb(st, 'g_xin%d' % i, [128, KC, TB], F32) for i in range(2)])
            sq = sb(st, 'g_sq', [128, KC, TB], BF16)
            b_sq = Buf()
            yT_r = Ring([sb(st, 'g_yT%d' % i, [128, KC, TB], F32) for i in range(2)])
            lnv = sb(st, 'g_lnv', [128, TB], F32)
            b_lnv = Buf()
            rstd = sb(st, 'g_rstd', [128, TB], F32)
            b_rstd = Buf()
            ot_r = Ring([sb(st, 'g_ot%d' % i, [128, D], F32) for i in range(3)])
            main_r = Ring(banks[0:6])
            main_r.b = bbank[0:6]
            ssb, b_ssb = (banks[6], bbank[6])
            b_out = []
            for tb in range(NTB):
                xin, bxin = xin_r.next()
                P.dma('sp', ('dma_start', dict(out=xin[:], in_=xT_d.rearrange('(c p) t -> p c t', p=128)[:, :, tb * TB:(tb + 1) * TB])), writes=[bxin])
                yT, byT = yT_r.next()
                rmsnorm_fm(xin, bxin, KC, TB, sq, b_sq, ssb, b_ssb, lnv, b_lnv, rstd, b_rstd, yT, byT, lambda c: col(0, 80 + c), 1.0 / D)
                for ts in range(4):
                    ot, bot = ot_r.next()
                    for half in range(2):
                        bk, bbk = main_r.next()
                        for j in range(4):
                            c = half * 4 + j
                            P.op('pe', ('transpose', dict(out=bk[:, j * 128:(j + 1) * 128], in_=yT[:, c, ts * 128:(ts + 1) * 128], identity=ident[:])), reads=[byT, B_const], writes=[bbk])
                        if half == 0:
                            P.op('dve', ('tensor_copy', dict(out=ot[:, 0:512], in_=bk[:])), reads=[bbk], writes=[bot])
                        else:
                            P.op('act', ('activation', dict(out=ot[:, 512:1024], in_=bk[:], func=AF.Copy)), reads=[bbk], writes=[bot])
                    bo = Buf()
                    b_out.append(bo)
                    r0 = tb * TB + ts * 128
                    P.dma('pool', ('dma_start', dict(out=out_d[r0:r0 + 128, :], in_=ot[:])), reads=[bot], writes=[bo])
            P.op('sp', None, reads=b_out)
        P.barrier()
        P.emit()
    return (nc, P)

def _colify(v):
    v = np.asarray(v, np.float32).reshape(-1, 128)
    return np.ascontiguousarray(v.T)

def prep_shared(inp, depth):
    f = lambda a: np.ascontiguousarray(np.asarray(a, np.float32))
    sh = {}
    for k in ('w_in', 'w_out', 'xa_wq', 'xa_wk', 'xa_wv', 'xa_wo', 'mlp_w1', 'mlp_w2'):
        sh[k] = f(inp[k])[:depth]
    cols = np.zeros((depth, 128, NCOL), np.float32)
    bd = np.zeros((depth, 128, 2, 2, 4, 128), np.float32)
    for l in range(depth):
        cols[l, :, 0:8] = _colify(inp['g_mix'][l])
        cols[l, :, 8:16] = _colify(inp['g_xq'][l])
        cols[l, :, 16:24] = _colify(inp['g_mem'][l])
        cols[l, :, 24:32] = _colify(inp['g_mlp'][l])
        cw = np.asarray(inp['lru_conv_w'][l], np.float32)
        for j in range(4):
            cols[l, :, 32 + j * 4:36 + j * 4] = _colify(cw[j])
        cols[l, :, 48:52] = _colify(inp['lru_conv_b'][l])
        for dr in range(2):
            cols[l, :, 52 + dr * 4:56 + dr * 4] = _colify(np.asarray(inp['lru_b_r'][l, dr]).reshape(-1))
            cols[l, :, 60 + dr * 4:64 + dr * 4] = _colify(np.asarray(inp['lru_b_i'][l, dr]).reshape(-1))
            cols[l, :, 68 + dr * 4:72 + dr * 4] = _colify(inp['lru_a_param'][l, dr])
        cols[l, :, 76:80] = _colify(inp['lru_norm_g'][l])
        cols[l, :, 80:88] = _colify(inp['g_final'])
        for dr in range(2):
            for g, key in enumerate(('lru_w_r', 'lru_w_i')):
                w = np.asarray(inp[key][l, dr], np.float32)
                for n in range(NB):
                    cc, half = (n // 2, n % 2)
                    bd[l, half * 64:(half + 1) * 64, dr, g, cc, half * 64:(half + 1) * 64] = w[n]
    sh['cols'] = cols
    sh['lru_bd'] = np.ascontiguousarray(bd.reshape(depth, 128, 16 * 128))
    lamv = np.stack([np.concatenate([np.asarray(inp[k][l], np.float32) for k in ('da_lq1', 'da_lk1', 'da_lq2', 'da_lk2')]) for l in range(depth)])
    sh['lamv'] = np.ascontiguousarray(lamv.reshape(depth, 1, 256))
    sh['gsub'] = np.ascontiguousarray(np.asarray(inp['da_subln_g'], np.float32)[:depth].reshape(depth, 128, 1))
    sh['ident'] = np.eye(128, dtype=np.float32)
    pm = np.zeros((128, 128), np.float32)
    fsc = np.zeros((128, 3), np.float32)
    inv_freq = np.power(np.float32(ROPE_THETA), -np.arange(0, 16, 2, dtype=np.float32) / np.float32(16)).astype(np.float32)
    for m in range(128):
        d = m % 64
        if d < 8:
            pm[m + 8, m] = 1.0
            fsc[m, 0] = inv_freq[d]
            fsc[m, 1] = -1.0
        elif d < 16:
            pm[m - 8, m] = 1.0
            fsc[m, 0] = inv_freq[d - 8]
            fsc[m, 1] = 1.0
    for m in range(128):
        fsc[m, 2] = inv_freq[m // 16]
    sh['pmat'] = pm
    sh['fscol'] = fsc
    return sh
_CACHE = {}

def kernel(**inputs):
    x = np.asarray(inputs['x'], np.float32)
    B, S, _ = x.shape
    depth = np.asarray(inputs['w_in']).shape[0]
    key = (S, depth)
    if key not in _CACHE:
        _CACHE[key] = build(S, depth, False)[0]
    nc = _CACHE[key]
    sh = prep_shared(inputs, depth)
    mem = np.asarray(inputs['mem'], np.float32)
    pos = np.asarray(inputs['positions'], np.int32)
    in_maps = []
    for b in range(B):
        m = dict(sh)
        m['x'] = np.ascontiguousarray(x[b])
        m['mem'] = np.ascontiguousarray(mem[b])
        m['pos'] = np.ascontiguousarray(pos[b].reshape(1, S))
        in_maps.append(m)
    res = run_bass_kernel_spmd(nc, in_maps, core_ids=list(range(B)))
    return np.stack([np.asarray(r['out'], np.float32) for r in res.results], axis=0)
```

```python
from contextlib import ExitStack
import numpy as np
import concourse.bass as bass
import concourse.mybir as mybir
from concourse.bass_utils import run_bass_kernel_spmd
F32 = mybir.dt.float32
BF16 = mybir.dt.bfloat16
I32 = mybir.dt.int32
AF = mybir.ActivationFunctionType
ALU = mybir.AluOpType
AX = mybir.AxisListType
D = 1024
KC = 8
NMEM = 256
DEPTH = 2
SEQ = 4096
NB = 8
EPS = 1e-06
TB = 512
NCOL = 88
ROPE_THETA = 500000.0
ENGS = ('pe', 'act', 'dve', 'pool', 'sp')
ATTACH_WAITS = True

class Buf:
    __slots__ = ('name', 'lw', 'rs')

    def __init__(self, name=''):
        self.name = name
        self.lw = None
        self.rs = []

class Op:
    __slots__ = ('eng', 'fn', 'dma', 'idx', 'deps', 'need', 'sem', 'val', 'pre')

    def __init__(self, eng, fn, dma, idx):
        self.eng = eng
        self.fn = fn
        self.dma = dma
        self.idx = idx
        self.deps = []
        self.need = False
        self.sem = None
        self.val = None
        self.pre = None

class Prog:

    def __init__(self, nc, n_dma_sems=24):
        self.nc = nc
        self.ops = []
        self.n_dma_sems = n_dma_sems
        self.since = []

    def op(self, eng, fn, reads=(), writes=(), dma=False):
        o = Op(eng, fn, dma, len(self.ops))
        deps = {}
        raw_src = set()
        for b in reads:
            if b.lw is not None:
                deps[b.lw.idx] = b.lw
                raw_src.add(b.lw.idx)
        for b in writes:
            if b.lw is not None:
                deps[b.lw.idx] = b.lw
            for r in b.rs:
                deps[r.idx] = r
        for b in reads:
            b.rs.append(o)
        for b in writes:
            b.lw = o
            b.rs = []
        for d in deps.values():
            if d is o:
                continue
            if not d.dma and (not dma) and (d.eng == eng):
                if eng == 'pe':
                    continue
                if d.idx not in raw_src:
                    continue
            o.deps.append(d)
            d.need = True
        self.ops.append(o)
        self.since.append(o)
        return o

    def dma(self, eng, fn, reads=(), writes=()):
        return self.op(eng, fn, reads, writes, dma=True)

    def barrier(self):
        last = {}
        dmas = []
        for o in self.since:
            if o.fn is None:
                continue
            if o.dma:
                dmas.append(o)
            else:
                last[o.eng] = o
        deps = list(last.values()) + dmas
        self.since = []
        for e in ENGS:
            w = Op(e, None, False, len(self.ops))
            for d in deps:
                if not d.dma and d.eng == e:
                    continue
                w.deps.append(d)
                d.need = True
            self.ops.append(w)

    def emit(self):
        nc = self.nc
        with ExitStack() as st:
            esem = {e: st.enter_context(nc.semaphore('s_' + e)) for e in ENGS}
            dsem = {e: [st.enter_context(nc.semaphore('d_%s_%d' % (e, i))) for i in range(self.n_dma_sems)] for e in ('sp', 'pool', 'act')}
            ecount = {e: 0 for e in ENGS}
            dcount = {e: [0] * self.n_dma_sems for e in dsem}
            drr = {e: 0 for e in dsem}
            for o in self.ops:
                if o.fn is None:
                    continue
                if o.dma:
                    k = drr[o.eng] % self.n_dma_sems
                    drr[o.eng] += 1
                    o.sem = dsem[o.eng][k]
                    o.pre = (o.sem, dcount[o.eng][k])
                    dcount[o.eng][k] += 16
                    o.val = dcount[o.eng][k]
                elif o.need:
                    ecount[o.eng] += 1
                    o.sem = esem[o.eng]
                    o.val = ecount[o.eng]
            per = {e: [o for o in self.ops if o.eng == e] for e in ENGS}
            self.stats = {e: len(per[e]) for e in ENGS}

            def run(e, eng):
                waited = {}
                for o in per[e]:
                    ws = [(d.sem, d.val) for d in o.deps if d.sem is not None]
                    if o.dma and o.pre[1] > 0:
                        ws.append(o.pre)
                    best = {}
                    for s, v in ws:
                        key = id(s)
                        if waited.get(key, 0) >= v:
                            continue
                        if key not in best or best[key][1] < v:
                            best[key] = (s, v)
                    items = list(best.values())
                    for s, v in items:
                        waited[id(s)] = v
                    attach = None
                    if o.fn is not None and items and ATTACH_WAITS:
                        attach = items.pop()
                    for s, v in items:
                        eng.wait_ge(s, v)
                    if o.fn is None:
                        continue
                    ins = getattr(eng, o.fn[0])(**o.fn[1])
                    if attach is not None:
                        ins._wait_ge(attach[0], attach[1])
                    if o.dma:
                        ins.then_inc(o.sem, 16)
                    elif o.need:
                        ins.then_inc(o.sem, 1)
            with nc.Block() as block:

                @block.sync
                def _(eng):
                    run('sp', eng)

                @block.tensor
                def _(eng):
                    run('pe', eng)

                @block.vector
                def _(eng):
                    run('dve', eng)

                @block.scalar
                def _(eng):
                    run('act', eng)

                @block.gpsimd
                def _(eng):
                    run('pool', eng)

class Ring:
    def __init__(self, tensors):
        self.t = tensors
        self.b = [Buf() for _ in tensors]
        self.i = 0

    def next(self):
        k = self.i % len(self.t)
        self.i += 1
        return (self.t[k], self.b[k])

def build(S=SEQ, depth=DEPTH, debug=False):
    NTB = S // TB
    SEG = S // 16
    NKB = S // 128
    nc = bass.Bass('TRN2', target_bir_lowering=False)
    P = Prog(nc)

    def dram_in(name, shape, dt=F32):
        return nc.dram_tensor(name, list(shape), dt, kind='ExternalInput').ap()
    skind = 'ExternalOutput' if debug else 'Internal'

    def dram_scr(name, shape, dt=F32):
        return nc.dram_tensor(name, list(shape), dt, kind=skind).ap()
    x_in = dram_in('x', [S, D])
    mem_in = dram_in('mem', [NMEM, D])
    pos_in = dram_in('pos', [1, S], I32)
    w_in_d = dram_in('w_in', [depth, D, 2560])
    w_out_d = dram_in('w_out', [depth, D, D])
    wq_d = dram_in('xa_wq', [depth, D, D])
    wk_d = dram_in('xa_wk', [depth, D, D])
    wv_d = dram_in('xa_wv', [depth, D, D])
    wo_d = dram_in('xa_wo', [depth, D, D])
    w1_d = dram_in('mlp_w1', [depth, D, 4 * D])
    w2_d = dram_in('mlp_w2', [depth, 4 * D, D])
    bd_d = dram_in('lru_bd', [depth, 128, 16 * 128])
    cols_d = dram_in('cols', [depth, 128, NCOL])
    lamv_d = dram_in('lamv', [depth, 1, 256])
    gsub_d = dram_in('gsub', [depth, 128, 1])
    ident_d = dram_in('ident', [128, 128])
    pmat_d = dram_in('pmat', [128, 128])
    fs_d = dram_in('fscol', [128, 3])
    out_d = nc.dram_tensor('out', [S, D], F32, kind='ExternalOutput').ap()
    xT_d = dram_scr('xT', [D, S])
    memT_d = dram_scr('memT', [D, NMEM])
    ctab_d = dram_scr('ctab', [128, S], BF16)
    stab_d = dram_scr('stab', [128, S], BF16)
    qk_d = dram_scr('qkT', [D, S], BF16)
    xb_d = dram_scr('xbT', [512, S])
    gg_d = dram_scr('ggT', [512, S])
    mix_d = dram_scr('mixT', [D, S], BF16)
    with ExitStack() as gst:

        uid = [0]

        def sb(st, name, shape, dt):
            uid[0] += 1
            return st.enter_context(nc.sbuf_tensor('sb%d_%s' % (uid[0], name), list(shape), dt))
        ident = sb(gst, 'ident', [128, 128], F32)
        identb = sb(gst, 'identb', [128, 128], BF16)
        pmat = sb(gst, 'pmat', [128, 128], BF16)
        ones = sb(gst, 'ones', [128, 128], BF16)
        fs = sb(gst, 'fs', [128, 3], F32)
        cc_ = sb(gst, 'cconst', [128, 4], F32)
        cols = sb(gst, 'cols', [128, depth * NCOL], F32)
        B_const = Buf('const')
        pairs = [gst.enter_context(nc.psum_tensor('pp%d' % i, [128, 2, 512], F32)) for i in range(4)]
        banks = [pairs[i // 2][:, i % 2, :] for i in range(8)]
        bbank = [Buf('bank%d' % i) for i in range(8)]
        P.dma('sp', ('dma_start', dict(out=ident[:], in_=ident_d[:, :])), writes=[B_const])
        P.dma('pool', ('dma_start', dict(out=identb[:], in_=ident_d[:, :])), writes=[B_const])
        P.dma('pool', ('dma_start', dict(out=pmat[:], in_=pmat_d[:, :])), writes=[B_const])
        P.dma('sp', ('dma_start', dict(out=fs[:], in_=fs_d[:, :])), writes=[B_const])
        for l in range(depth):
            P.dma('sp', ('dma_start', dict(out=cols[:, l * NCOL:(l + 1) * NCOL], in_=cols_d[l])), writes=[B_const])
        P.op('dve', ('memset', dict(ap=ones[:], constant=1.0)), writes=[B_const])
        P.op('dve', ('memset', dict(ap=cc_[:, 0:1], constant=EPS)), writes=[B_const])
        P.op('dve', ('memset', dict(ap=cc_[:, 1:2], constant=1.0)), writes=[B_const])
        P.op('dve', ('memset', dict(ap=cc_[:, 2:3], constant=0.0)), writes=[B_const])
        eps_c = cc_[:, 0:1]
        one_c = cc_[:, 1:2]
        P.barrier()

        def col(l, j):
            return cols[:, l * NCOL + j:l * NCOL + j + 1]

        def norm1(xin, bx, nch, N, sq, bsq):
            P.op('act', ('activation', dict(out=sq[:, 0:nch, 0:N], in_=xin[:, 0:nch, 0:N], func=AF.Square)), reads=[bx], writes=[bsq])

        def norm2(xin, bx, nch, N, sq, bsq, ssbank, bss, lnv, blnv, rstd, brstd, hT, bhT, gcol_fn, inv_n):
            for c in range(nch):
                P.op('pe', ('matmul', dict(out=ssbank[:, 0:N], lhsT=ones[:], rhs=sq[:, c, 0:N], start=c == 0, stop=c == nch - 1)), reads=[bsq, B_const], writes=[bss])
            P.op('act', ('activation', dict(out=lnv[:, 0:N], in_=ssbank[:, 0:N], func=AF.Ln, bias=eps_c, scale=inv_n)), reads=[bss], writes=[blnv])
            P.op('act', ('activation', dict(out=rstd[:, 0:N], in_=lnv[:, 0:N], func=AF.Exp, scale=-0.5)), reads=[blnv], writes=[brstd])
            for c in range(nch):
                P.op('dve', ('scalar_tensor_tensor', dict(out=hT[:, c, 0:N], in0=xin[:, c, 0:N], scalar=gcol_fn(c), in1=rstd[:, 0:N], op0=ALU.mult, op1=ALU.mult)), reads=[bx, brstd], writes=[bhT])

        def rmsnorm_fm(xin, bx, nch, N, sq, bsq, ssbank, bss, lnv, blnv, rstd, brstd, hT, bhT, gcol_fn, inv_n):
            norm1(xin, bx, nch, N, sq, bsq)
            norm2(xin, bx, nch, N, sq, bsq, ssbank, bss, lnv, blnv, rstd, brstd, hT, bhT, gcol_fn, inv_n)
        with ExitStack() as st:
            xtok = [sb(st, 'p_xtok%d' % i, [128, D], F32) for i in range(2)]
            r_xtok = Ring(xtok)
            stg = [sb(st, 'p_stg%d' % i, [128, KC, TB], F32) for i in range(2)]
            r_stg = Ring(stg)
            r_bank = Ring(banks[0:4])
            r_bank.b = bbank[0:4]
            for tb in range(NTB):
                sg, bsg = r_stg.next()
                toks = []
                for ts in range(4):
                    xt, bxt = r_xtok.next()
                    r0 = tb * TB + ts * 128
                    P.dma('sp', ('dma_start', dict(out=xt[:], in_=x_in[r0:r0 + 128, :])), writes=[bxt])
                    toks.append((xt, bxt))
                    for c in range(KC):
                        pass
                    for half in range(2):
                        bk, bbk = r_bank.next()
                        for j in range(4):
                            c = half * 4 + j
                            P.op('pe', ('transpose', dict(out=bk[:, j * 128:(j + 1) * 128], in_=xt[:, c * 128:(c + 1) * 128], identity=ident[:])), reads=[bxt, B_const], writes=[bbk])
                        eng = 'dve' if half == 0 else 'act'
                        if eng == 'dve':
                            P.op('dve', ('tensor_copy', dict(out=sg[:, half * 4:half * 4 + 4, ts * 128:(ts + 1) * 128], in_=bk[:].rearrange('p (c t) -> p c t', c=4))), reads=[bbk], writes=[bsg])
                        else:
                            P.op('act', ('activation', dict(out=sg[:, half * 4:half * 4 + 4, ts * 128:(ts + 1) * 128], in_=bk[:].rearrange('p (c t) -> p c t', c=4), func=AF.Copy)), reads=[bbk], writes=[bsg])
                P.dma('pool', ('dma_start', dict(out=xT_d.rearrange('(c p) t -> p c t', p=128)[:, :, tb * TB:(tb + 1) * TB], in_=sg[:])), reads=[bsg])
            sg, bsg = r_stg.next()
            for mb in range(2):
                xt, bxt = r_xtok.next()
                P.dma('sp', ('dma_start', dict(out=xt[:], in_=mem_in[mb * 128:(mb + 1) * 128, :])), writes=[bxt])
                for half in range(2):
                    bk, bbk = r_bank.next()
                    for j in range(4):
                        c = half * 4 + j
                        P.op('pe', ('transpose', dict(out=bk[:, j * 128:(j + 1) * 128], in_=xt[:, c * 128:(c + 1) * 128], identity=ident[:])), reads=[bxt, B_const], writes=[bbk])
                    P.op('dve', ('tensor_copy', dict(out=sg[:, half * 4:half * 4 + 4, mb * 128:(mb + 1) * 128], in_=bk[:].rearrange('p (c t) -> p c t', c=4))), reads=[bbk], writes=[bsg])
            P.dma('pool', ('dma_start', dict(out=memT_d.rearrange('(c p) t -> p c t', p=128), in_=sg[:, :, 0:NMEM])), reads=[bsg])
            posi = sb(st, 'p_posi', [128, SEG], I32)
            ang = sb(st, 'p_ang', [128, SEG], F32)
            t1 = sb(st, 'p_t1', [128, SEG], F32)
            t2 = sb(st, 'p_t2', [128, SEG], F32)
            ni = sb(st, 'p_ni', [128, SEG], I32)
            tabc = sb(st, 'p_tabc', [128, SEG], BF16)
            tabs = sb(st, 'p_tabs', [128, SEG], BF16)
            tabn = sb(st, 'p_tabn', [128, SEG], BF16)
            cfill = sb(st, 'p_cfill', [128, S], BF16)
            zfill = sb(st, 'p_zfill', [128, S], BF16)
            b_ctd = Buf('ctab_d')
            b_std = Buf('stab_d')
            b_fill = Buf('fill')
            bR = Buf('rope')
            TWO_PI = 2.0 * np.pi
            C1 = 6.28125
            C2 = TWO_PI - C1
            for j in range(8):
                P.dma('sp', ('dma_start', dict(out=posi[j * 16:(j + 1) * 16, :], in_=pos_in.rearrange('o (g t) -> (o g) t', t=SEG))), writes=[bR])
            P.op('pool', ('memset', dict(ap=cfill[:], constant=1.0)), writes=[b_fill])
            P.op('pool', ('memset', dict(ap=zfill[:], constant=0.0)), writes=[b_fill])
            P.dma('pool', ('dma_start', dict(out=ctab_d[:, :], in_=cfill[:])), reads=[b_fill], writes=[b_ctd])
            P.dma('pool', ('dma_start', dict(out=stab_d[:, :], in_=zfill[:])), reads=[b_fill], writes=[b_std])
            P.op('dve', ('tensor_copy', dict(out=t1[:], in_=posi[:])), reads=[bR], writes=[bR])
            P.op('dve', ('tensor_scalar', dict(out=ang[:], in0=t1[:], scalar1=fs[:, 2:3], scalar2=None, op0=ALU.mult)), reads=[bR, B_const], writes=[bR])

            def reduce_sin(src_shift, dst_bf, signcol):
                P.op('dve', ('tensor_scalar', dict(out=t1[:], in0=ang[:], scalar1=float(src_shift), scalar2=None, op0=ALU.add)), reads=[bR], writes=[bR])
                P.op('dve', ('tensor_scalar', dict(out=t2[:], in0=t1[:], scalar1=float(1.0 / TWO_PI), scalar2=None, op0=ALU.mult)), reads=[bR], writes=[bR])
                P.op('dve', ('tensor_copy', dict(out=ni[:], in_=t2[:])), reads=[bR], writes=[bR])
                P.op('dve', ('tensor_copy', dict(out=t2[:], in_=ni[:])), reads=[bR], writes=[bR])
                P.op('dve', ('scalar_tensor_tensor', dict(out=t1[:], in0=t2[:], scalar=float(-C1), in1=t1[:], op0=ALU.mult, op1=ALU.add)), reads=[bR], writes=[bR])
                P.op('dve', ('scalar_tensor_tensor', dict(out=t1[:], in0=t2[:], scalar=float(-C2), in1=t1[:], op0=ALU.mult, op1=ALU.add)), reads=[bR], writes=[bR])
                P.op('dve', ('tensor_scalar', dict(out=t2[:], in0=t1[:], scalar1=float(np.pi), scalar2=float(-TWO_PI), op0=ALU.is_gt, op1=ALU.mult)), reads=[bR], writes=[bR])
                P.op('dve', ('tensor_tensor', dict(out=t1[:], in0=t1[:], in1=t2[:], op=ALU.add)), reads=[bR], writes=[bR])
                P.op('dve', ('tensor_scalar', dict(out=t2[:], in0=t1[:], scalar1=float(-np.pi), scalar2=float(TWO_PI), op0=ALU.is_lt, op1=ALU.mult)), reads=[bR], writes=[bR])
                P.op('dve', ('tensor_tensor', dict(out=t1[:], in0=t1[:], in1=t2[:], op=ALU.add)), reads=[bR], writes=[bR])
                P.op('dve', ('tensor_scalar', dict(out=t1[:], in0=t1[:], scalar1=float(np.pi - 1e-06), scalar2=float(-np.pi + 1e-06), op0=ALU.min, op1=ALU.max)), reads=[bR], writes=[bR])
                P.op('act', ('activation', dict(out=t2[:], in_=t1[:], func=AF.Sin)), reads=[bR], writes=[bR])
                if signcol is None:
                    P.op('dve', ('tensor_copy', dict(out=dst_bf[:], in_=t2[:])), reads=[bR], writes=[bR])
                else:
                    P.op('dve', ('tensor_scalar', dict(out=dst_bf[:], in0=t2[:], scalar1=signcol, scalar2=None, op0=ALU.mult)), reads=[bR, B_const], writes=[bR])
            reduce_sin(np.pi / 2.0, tabc, None)
            reduce_sin(0.0, tabs, None)
            P.op('dve', ('tensor_scalar', dict(out=tabn[:], in0=tabs[:], scalar1=-1.0, scalar2=None, op0=ALU.mult)), reads=[bR], writes=[bR])
            for r0 in (0, 8, 64, 72):
                P.dma('sp', ('dma_start', dict(out=ctab_d[r0:r0 + 8, :].rearrange('j (g t) -> (j g) t', t=SEG), in_=tabc[:])), reads=[bR], writes=[b_ctd])
            for r0 in (0, 64):
                P.dma('sp', ('dma_start', dict(out=stab_d[r0:r0 + 8, :].rearrange('j (g t) -> (j g) t', t=SEG), in_=tabn[:])), reads=[bR], writes=[b_std])
            for r0 in (8, 72):
                P.dma('sp', ('dma_start', dict(out=stab_d[r0:r0 + 8, :].rearrange('j (g t) -> (j g) t', t=SEG), in_=tabs[:])), reads=[bR], writes=[b_std])
        P.barrier()
        for l in range(depth):
            lam_init = 0.8 - 0.6 * float(np.exp(-0.3 * l))
            with ExitStack() as lst:
                v_all = sb(lst, 'v_all', [128, NKB, 4, 129], BF16)
                b_vall = Buf('v_all')
                with ExitStack() as st:
                    win = sb(st, 'a_win', [128, KC, 2560], BF16)
                    b_win = [Buf() for _ in range(KC)]
                    ctab = sb(st, 'a_ctab', [128, S], BF16)
                    stab = sb(st, 'a_stab', [128, S], BF16)
                    b_tab = Buf()
                    xin_r = Ring([sb(st, 'a_xin%d' % i, [128, KC, TB], F32) for i in range(2)])
                    sq = sb(st, 'a_sq', [128, KC, TB], BF16)
                    b_sq = Buf()
                    hT_r = Ring([sb(st, 'a_hT%d' % i, [128, KC, TB], BF16) for i in range(2)])
                    lnv = sb(st, 'a_lnv', [128, TB], F32)
                    b_lnv = Buf()
                    rstd = sb(st, 'a_rstd', [128, TB], F32)
                    b_rstd = Buf()
                    tbf_r = Ring([sb(st, 'a_tbf%d' % i, [128, TB], BF16) for i in range(2)])
                    tm1_r = Ring([sb(st, 'a_tm1%d' % i, [128, TB], F32) for i in range(2)])
                    tm2_r = Ring([sb(st, 'a_tm2%d' % i, [128, TB], F32) for i in range(2)])
                    qk_r = Ring([sb(st, 'a_qk%d' % i, [128, 8, TB], BF16) for i in range(2)])
                    xg_r = Ring([sb(st, 'a_xg%d' % i, [128, 8, TB], F32) for i in range(1)])
                    main_r = Ring(banks[0:4])
                    main_r.b = bbank[0:4]
                    sw_r = Ring(banks[5:7])
                    sw_r.b = bbank[5:7]
                    ssb, b_ssb = (banks[4], bbank[4])
                    for kc in range(KC):
                        P.dma('pool', ('dma_start', dict(out=win[:, kc, :], in_=w_in_d[l, kc * 128:(kc + 1) * 128, :])), writes=[b_win[kc]])
                    P.dma('sp', ('dma_start', dict(out=ctab[:], in_=ctab_d[:, :])), writes=[b_tab])
                    P.dma('sp', ('dma_start', dict(out=stab[:], in_=stab_d[:, :])), writes=[b_tab])
                    P.op('dve', ('memset', dict(ap=v_all[:, :, :, 128:129], constant=1.0)), writes=[b_vall])

                    def load_x(tb):
                        xin, bxin = xin_r.next()
                        P.dma('sp', ('dma_start', dict(out=xin[:], in_=xT_d.rearrange('(c p) t -> p c t', p=128)[:, :, tb * TB:(tb + 1) * TB])), writes=[bxin])
                        return (xin, bxin)
                    sq_r = Ring([sq, sb(st, 'a_sq2', [128, KC, TB], BF16)])
                    cur = load_x(0)
                    hcur = hT_r.next()
                    sqc = sq_r.next()
                    norm1(cur[0], cur[1], KC, TB, sqc[0], sqc[1])
                    norm2(cur[0], cur[1], KC, TB, sqc[0], sqc[1], ssb, b_ssb, lnv, b_lnv, rstd, b_rstd, hcur[0], hcur[1], lambda c: col(l, 0 + c), 1.0 / D)
                    for tb in range(NTB):
                        hT, bhT = hcur
                        if tb + 1 < NTB:
                            nxt = load_x(tb + 1)
                            hnx = hT_r.next()
                            sqn = sq_r.next()
                            norm1(nxt[0], nxt[1], KC, TB, sqn[0], sqn[1])
                        qk, bqk = qk_r.next()
                        xg, bxg = xg_r.next()
                        tsl = slice(tb * TB, (tb + 1) * TB)
                        pend = None

                        def rope_tail(pd):
                            oc_, tbf, btbf = pd
                            tm1, btm1 = tm1_r.next()
                            tm2, btm2 = tm2_r.next()
                            sw, bsw = sw_r.next()
                            P.op('pe', ('matmul', dict(out=sw[:], lhsT=pmat[:], rhs=tbf[:], start=True, stop=True)), reads=[btbf, B_const], writes=[bsw])
                            P.op('dve', ('tensor_tensor', dict(out=tm1[:], in0=sw[:], in1=stab[:, tsl], op=ALU.mult)), reads=[bsw, b_tab], writes=[btm1])
                            P.op('pool', ('tensor_tensor', dict(out=tm2[:], in0=tbf[:], in1=ctab[:, tsl], op=ALU.mult)), reads=[btbf, b_tab], writes=[btm2])
                            P.op('dve', ('tensor_tensor', dict(out=qk[:, oc_, :], in0=tm1[:], in1=tm2[:], op=ALU.add)), reads=[btm1, btm2], writes=[bqk])
                        chunks = list(range(0, 8)) + list(range(12, 20))
                        for ci, oc in enumerate(chunks):
                            bk, bbk = main_r.next()
                            for kc in range(KC):
                                P.op('pe', ('matmul', dict(out=bk[:], lhsT=win[:, kc, oc * 128:(oc + 1) * 128], rhs=hT[:, kc, :], start=kc == 0, stop=kc == KC - 1)), reads=[b_win[kc], bhT], writes=[bbk])
                            if pend is not None:
                                rope_tail(pend)
                                pend = None
                            if oc < 8:
                                tbf, btbf = tbf_r.next()
                                P.op('act', ('activation', dict(out=tbf[:], in_=bk[:], func=AF.Copy)), reads=[bbk], writes=[btbf])
                                pend = (oc, tbf, btbf)
                            elif oc < 16:
                                P.op('dve', ('tensor_copy', dict(out=xg[:, oc - 12, :], in_=bk[:])), reads=[bbk], writes=[bxg])
                            else:
                                P.op('act', ('activation', dict(out=xg[:, oc - 12, :], in_=bk[:], func=AF.Gelu)), reads=[bbk], writes=[bxg])
                            if ci == 11 and tb + 1 < NTB:
                                norm2(nxt[0], nxt[1], KC, TB, sqn[0], sqn[1], ssb, b_ssb, lnv, b_lnv, rstd, b_rstd, hnx[0], hnx[1], lambda c: col(l, 0 + c), 1.0 / D)
                        for ts in range(4):
                            bk, bbk = main_r.next()
                            for kc in range(KC):
                                P.op('pe', ('matmul', dict(out=bk[:], lhsT=hT[:, kc, ts * 128:(ts + 1) * 128], rhs=win[:, kc, 1024:1536], start=kc == 0, stop=kc == KC - 1)), reads=[b_win[kc], bhT], writes=[bbk])
                            kb = tb * 4 + ts
                            P.op('act', ('activation', dict(out=v_all[:, kb, :, 0:128], in_=bk[:].rearrange('p (h e) -> p h e', h=4), func=AF.Copy)), reads=[bbk], writes=[b_vall])
                        if tb + 1 < NTB:
                            hcur = hnx
                        P.dma('pool', ('dma_start', dict(out=qk_d.rearrange('(c p) t -> p c t', p=128)[:, :, tb * TB:(tb + 1) * TB], in_=qk[:])), reads=[bqk])
                        P.dma('pool', ('dma_start', dict(out=xb_d.rearrange('(c p) t -> p c t', p=128)[:, :, tb * TB:(tb + 1) * TB], in_=xg[:, 0:4, :])), reads=[bxg])
                        P.dma('pool', ('dma_start', dict(out=gg_d.rearrange('(c p) t -> p c t', p=128)[:, :, tb * TB:(tb + 1) * TB], in_=xg[:, 4:8, :])), reads=[bxg])
                P.barrier()
                with ExitStack() as st:
                    qh_r = Ring([sb(st, 'b_qh%d' % i, [128, S], BF16) for i in range(2)])
                    kh_r = Ring([sb(st, 'b_kh%d' % i, [128, S], BF16) for i in range(2)])
                    pt_r = Ring([sb(st, 'b_pt%d' % i, [128, 2, TB], BF16) for i in range(5)])
                    ps2_r = Ring([sb(st, 'b_ps2%d' % i, [128, 2, TB], BF16) for i in range(6)])
                    lamv = sb(st, 'b_lamv', [128, 256], F32)
                    lamt = sb(st, 'b_lamt', [128, 128], F32)
                    lams = sb(st, 'b_lams', [128, 8], F32)
                    gsubc = sb(st, 'b_gsubc', [128, 1], F32)
                    b_lam = Buf()
                    oc_r = Ring([sb(st, 'b_oc%d' % i, [128, 4, TB], F32) for i in range(2)])
                    rd = sb(st, 'b_rd', [128, 2, TB], F32)
                    b_rd = Buf()
                    o1 = sb(st, 'b_o1', [128, TB], F32)
                    b_o1 = Buf()
                    o2 = sb(st, 'b_o2', [128, TB], F32)
                    b_o2 = Buf()
                    oo_r = Ring([sb(st, 'b_oo%d' % i, [128, TB], F32) for i in range(2)])
                    osq_r = Ring([sb(st, 'b_osq%d' % i, [128, TB], BF16) for i in range(2)])
                    lnb = sb(st, 'b_lnb', [128, TB], F32)
                    b_lnb = Buf()
                    rsb = sb(st, 'b_rsb', [128, TB], F32)
                    b_rsb = Buf()
                    oT_r = Ring([sb(st, 'b_oT%d' % i, [128, TB], BF16) for i in range(2)])
                    s_r = Ring(pairs[0:2])
                    accp = pairs[2:4]
                    b_acc = [Buf(), Buf()]
                    P.dma('sp', ('dma_start', dict(out=lamv[:], in_=lamv_d[l].to_broadcast([128, 256]))), writes=[b_lam])
                    P.dma('sp', ('dma_start', dict(out=gsubc[:], in_=gsub_d[l])), writes=[b_lam])
                    P.op('dve', ('tensor_tensor', dict(out=lamt[:, 0:64], in0=lamv[:, 0:64], in1=lamv[:, 64:128], op=ALU.mult)), reads=[b_lam], writes=[b_lam])
                    P.op('dve', ('tensor_tensor', dict(out=lamt[:, 64:128], in0=lamv[:, 128:192], in1=lamv[:, 192:256], op=ALU.mult)), reads=[b_lam], writes=[b_lam])
                    P.op('dve', ('tensor_reduce', dict(out=lams[:, 0:2], in_=lamt[:].rearrange('p (a b) -> p a b', a=2), axis=AX.X, op=ALU.add)), reads=[b_lam], writes=[b_lam])
                    P.op('act', ('activation', dict(out=lams[:, 2:4], in_=lams[:, 0:2], func=AF.Exp)), reads=[b_lam], writes=[b_lam])
                    P.op('dve', ('tensor_tensor', dict(out=lams[:, 4:5], in0=lams[:, 3:4], in1=lams[:, 2:3], op=ALU.subtract)), reads=[b_lam], writes=[b_lam])
                    P.op('dve', ('tensor_scalar', dict(out=lams[:, 5:6], in0=lams[:, 4:5], scalar1=float(-lam_init), scalar2=None, op0=ALU.add)), reads=[b_lam], writes=[b_lam])
                    P.op('dve', ('tensor_scalar', dict(out=gsubc[:], in0=gsubc[:], scalar1=float(1.0 - lam_init), scalar2=None, op0=ALU.mult)), reads=[b_lam], writes=[b_lam])
                    nlam = lams[:, 5:6]

                    def load_head(h):
                        qh, bqh = qh_r.next()
                        kh, bkh = kh_r.next()
                        P.dma('sp', ('dma_start', dict(out=qh[:], in_=qk_d[h * 128:(h + 1) * 128, :])), writes=[bqh])
                        P.dma('sp', ('dma_start', dict(out=kh[:], in_=qk_d[512 + h * 128:512 + (h + 1) * 128, :])), writes=[bkh])
                        return (qh, bqh, kh, bkh)

                    sel = sb(st, 'b_sel', [64, 2, 128], F32)
                    dsb = sb(st, 'b_dsb', [64, TB], F32)
                    b_dsb = Buf()
                    rd64 = sb(st, 'b_rd64', [64, TB], F32)
                    b_rd64 = Buf()
                    oc2_r = Ring([sb(st, 'b_oc2%d' % i, [128, 2, TB], F32) for i in range(2)])
                    dbank = pairs[3][:, 0, :]
                    b_accd = Buf()
                    P.op('dve', ('memset', dict(ap=sel[:], constant=0.0)), writes=[b_lam])
                    P.op('dve', ('memset', dict(ap=sel[0:1, 0, :], constant=1.0)), writes=[b_lam])
                    P.op('dve', ('memset', dict(ap=sel[32:33, 1, :], constant=1.0)), writes=[b_lam])

                    def epiA(h, qb):
                        oc, boc = oc2_r.next()
                        P.op('dve', ('tensor_copy', dict(out=oc[:], in_=accp[0][:])), reads=[b_acc[0]], writes=[boc])
                        P.op('dve', ('tensor_copy', dict(out=dsb[:], in_=dbank[0:64, :])), reads=[b_accd], writes=[b_dsb])
                        P.op('dve', ('reciprocal', dict(out=rd64[:], in_=dsb[:])), reads=[b_dsb], writes=[b_rd64])
                        return (h, qb, oc, boc)

                    def epiB(h, qb, oc, boc):
                        sp, bsp = s_r.next()
                        for c in range(2):
                            P.op('pe', ('matmul', dict(out=sp[:, c, :], lhsT=sel[:, c, :], rhs=rd64[:], start=True, stop=True)), reads=[b_rd64, b_lam], writes=[bsp])
                        P.op('dve', ('tensor_tensor', dict(out=o1[:], in0=oc[:, 0, :], in1=sp[:, 0, :], op=ALU.mult)), reads=[boc, bsp], writes=[b_o1])
                        P.op('dve', ('tensor_tensor', dict(out=o2[:], in0=oc[:, 1, :], in1=sp[:, 1, :], op=ALU.mult)), reads=[boc, bsp], writes=[b_o2])
                        oo, boo = oo_r.next()
                        P.op('dve', ('scalar_tensor_tensor', dict(out=oo[:], in0=o2[:], scalar=nlam, in1=o1[:], op0=ALU.mult, op1=ALU.add)), reads=[b_o1, b_o2, b_lam], writes=[boo])
                        osq, bosq = osq_r.next()
                        P.op('pool', ('tensor_tensor', dict(out=osq[:], in0=oo[:], in1=oo[:], op=ALU.mult)), reads=[boo], writes=[bosq])
                        return (h, qb, oo, boo, osq, bosq)

                    def epi2(h, qb, oo, boo, osq, bosq):
                        sp, bsp = s_r.next()
                        P.op('pe', ('matmul', dict(out=sp[:, 0, :], lhsT=ones[:], rhs=osq[:], start=True, stop=True)), reads=[bosq, B_const], writes=[bsp])
                        P.op('act', ('activation', dict(out=lnb[:], in_=sp[:, 0, :], func=AF.Ln, bias=eps_c, scale=1.0 / 128.0)), reads=[bsp], writes=[b_lnb])
                        P.op('act', ('activation', dict(out=rsb[:], in_=lnb[:], func=AF.Exp, scale=-0.5)), reads=[b_lnb], writes=[b_rsb])
                        oT, boT = oT_r.next()
                        P.op('dve', ('scalar_tensor_tensor', dict(out=oT[:], in0=oo[:], scalar=gsubc[:, 0:1], in1=rsb[:], op0=ALU.mult, op1=ALU.mult)), reads=[boo, b_rsb, b_lam], writes=[boT])
                        P.dma('pool', ('dma_start', dict(out=mix_d[h * 128:(h + 1) * 128, qb * TB:(qb + 1) * TB], in_=oT[:])), reads=[boT])

                    def qk(qh, bqh, kh, bkh, qb, kb):
                        sp, bsp = s_r.next()
                        for c in range(2):
                            P.op('pe', ('matmul', dict(out=sp[:, c, :], lhsT=kh[c * 64:(c + 1) * 64, kb * 128:(kb + 1) * 128], rhs=qh[c * 64:(c + 1) * 64, qb * TB:(qb + 1) * TB], start=True, stop=True)), reads=[bqh, bkh], writes=[bsp])
                        pt, bpt = pt_r.next()
                        P.op('act', ('activation', dict(out=pt[:], in_=sp[:], func=AF.Exp, scale=0.125)), reads=[bsp], writes=[bpt])
                        return (pt, bpt)

                    def pv(h, kb, pt, bpt):
                        for c in range(2):
                            P.op('pe', ('matmul', dict(out=accp[0][:, c, :], lhsT=v_all[:, kb, h, 0:128], rhs=pt[:, c, :], start=kb == 0, stop=kb == NKB - 1)), reads=[bpt, b_vall], writes=[b_acc[0]])

                    def den(ps2, bps2, pair):
                        for c in range(2):
                            P.op('pe', ('matmul', dict(out=dbank[c * 32:(c + 1) * 32, :], lhsT=ones[:, 0:32], rhs=ps2[:, c, :], start=pair == 0, stop=pair == NKB // 4 - 1)), reads=[bps2, B_const], writes=[b_accd])
                    pendB = None
                    pendC = None
                    tB = NKB // 4
                    tC = 3 * NKB // 4
                    nxth = load_head(0)
                    for h in range(4):
                        qh, bqh, kh, bkh = nxth
                        if h + 1 < 4:
                            nxth = load_head(h + 1)
                        for qb in range(NTB):
                            qq = [qk(qh, bqh, kh, bkh, qb, 0), qk(qh, bqh, kh, bkh, qb, 1)]
                            prev = None
                            pden = None
                            for kb in range(NKB):
                                if kb + 2 < NKB:
                                    qq.append(qk(qh, bqh, kh, bkh, qb, kb + 2))
                                cur = qq[kb]
                                nx = None
                                pv(h, kb, *cur)
                                if pden is not None:
                                    den(*pden)
                                    pden = None
                                if kb % 2 == 1:
                                    ps2, bps2 = ps2_r.next()
                                    P.op('dve', ('tensor_tensor', dict(out=ps2[:], in0=prev[0][:], in1=cur[0][:], op=ALU.add)), reads=[prev[1], cur[1]], writes=[bps2])
                                    if kb % 4 == 1:
                                        ps2a = (ps2, bps2)
                                    else:
                                        ps4, bps4 = ps2_r.next()
                                        P.op('dve', ('tensor_tensor', dict(out=ps4[:], in0=ps2a[0][:], in1=ps2[:], op=ALU.add)), reads=[ps2a[1], bps2], writes=[bps4])
                                        pden = (ps4, bps4, kb // 4)
                                prev = cur
                                if kb == tB and pendB is not None:
                                    pendC = epiB(*pendB)
                                    pendB = None
                                if kb == tC and pendC is not None:
                                    epi2(*pendC)
                                    pendC = None
                            den(*pden)
                            pendB = epiA(h, qb)
                    pendC = epiB(*pendB)
                    epi2(*pendC)
                P.barrier()
            with ExitStack() as st:
                bd = sb(st, 'c_bd', [128, 16, 128], BF16)
                b_bd = Buf()
                cbf = sb(st, 'c_cbf', [128, 4, S], BF16)
                b_cbf = [Buf() for _ in range(NTB)]
                hf = sb(st, 'c_hf', [128, 4, S], F32)
                b_hf = [Buf() for _ in range(NTB)]
                xb_r = Ring([sb(st, 'c_xb%d' % i, [128, 4, TB + 3], F32) for i in range(2)])
                ctmp = sb(st, 'c_ctmp', [128, 1, TB], F32)
                b_ctmp = Buf()
                gr = [sb(st, 'c_r%d' % i, [128, TB], F32) for i in range(4)]
                gi = [sb(st, 'c_i%d' % i, [128, TB], F32) for i in range(4)]
                ga = [sb(st, 'c_a%d' % i, [128, TB], F32) for i in range(4)]
                gt = gr
                gu = gi
                b_gr = [Buf() for _ in range(4)]
                b_gi = [Buf() for _ in range(4)]
                b_ga = [Buf() for _ in range(4)]
                b_gt = b_gr
                b_gu = b_gi
                hb_r = Ring([sb(st, 'c_hb%d' % i, [128, 4, TB], F32) for i in range(2)])
                gg_r = Ring([sb(st, 'c_gg%d' % i, [128, 4, TB], F32) for i in range(2)])
                yy = sb(st, 'c_y', [128, 4, TB], F32)
                b_yy = Buf()
                ysq = sb(st, 'c_ysq', [128, 4, TB], BF16)
                b_ysq = Buf()
                lnv = sb(st, 'c_lnv', [128, TB], F32)
                b_lnv = Buf()
                rstd = sb(st, 'c_rstd', [128, TB], F32)
                b_rstd = Buf()
                rT_r = Ring([sb(st, 'c_rT%d' % i, [128, 4, TB], BF16) for i in range(1)])
                nsp = sb(st, 'c_nsp', [128, 16], F32)
                b_nsp = Buf()
                g_r = Ring(banks[0:6])
                g_r.b = bbank[0:6]
                ssb, b_ssb = (banks[6], bbank[6])
                P.dma('pool', ('dma_start', dict(out=bd[:], in_=bd_d[l].rearrange('k (j m) -> k j m', j=16))), writes=[b_bd])
                P.op('act', ('activation', dict(out=nsp[:, 0:8], in_=cols[:, l * NCOL + 68:l * NCOL + 76], func=AF.Exp, scale=-1.0)), reads=[B_const], writes=[b_nsp])
                P.op('act', ('activation', dict(out=nsp[:, 0:8], in_=nsp[:, 0:8], func=AF.Ln, bias=one_c, scale=1.0)), reads=[b_nsp], writes=[b_nsp])
                P.op('dve', ('tensor_scalar', dict(out=nsp[:, 8:16], in0=nsp[:, 0:8], scalar1=-16.0, scalar2=None, op0=ALU.mult)), reads=[b_nsp], writes=[b_nsp])
                P.op('dve', ('tensor_scalar', dict(out=nsp[:, 0:8], in0=nsp[:, 0:8], scalar1=-8.0, scalar2=None, op0=ALU.mult)), reads=[b_nsp], writes=[b_nsp])

                def gates(dr, tb, out_fn, bout, init_fn, rev):
                    tsl = slice(tb * TB, (tb + 1) * TB)
                    for cc in range(4):
                        for g, (dst, bdst, cbase) in enumerate(((gr, b_gr, 52), (gi, b_gi, 60))):
                            bk, bbk = g_r.next()
                            P.op('pe', ('matmul', dict(out=bk[:], lhsT=bd[:, dr * 8 + g * 4 + cc, :], rhs=cbf[:, cc, tsl], start=True, stop=True)), reads=[b_bd, b_cbf[tb]], writes=[bbk])
                            P.op('act', ('activation', dict(out=dst[cc][:], in_=bk[:], func=AF.Sigmoid, bias=col(l, cbase + dr * 4 + cc))), reads=[bbk, B_const], writes=[bdst[cc]])
                    for cc in range(4):
                        P.op('act', ('activation', dict(out=ga[cc][:], in_=gr[cc][:], func=AF.Exp, scale=nsp[:, dr * 4 + cc:dr * 4 + cc + 1])), reads=[b_gr[cc], b_nsp], writes=[b_ga[cc]])
                        P.op('act', ('activation', dict(out=gt[cc][:], in_=gr[cc][:], func=AF.Exp, scale=nsp[:, 8 + dr * 4 + cc:8 + dr * 4 + cc + 1])), reads=[b_gr[cc], b_nsp], writes=[b_gt[cc]])
                        P.op('act', ('activation', dict(out=gt[cc][:], in_=gt[cc][:], func=AF.Ln, bias=one_c, scale=-1.0)), reads=[b_gt[cc]], writes=[b_gt[cc]])
                        P.op('act', ('activation', dict(out=gt[cc][:], in_=gt[cc][:], func=AF.Exp, scale=0.5)), reads=[b_gt[cc]], writes=[b_gt[cc]])
                        P.op('pool', ('tensor_tensor', dict(out=gu[cc][:], in0=cbf[:, cc, tsl], in1=gi[cc][:], op=ALU.mult)), reads=[b_cbf[tb], b_gi[cc]], writes=[b_gu[cc]])
                        P.op('dve', ('tensor_tensor', dict(out=gu[cc][:], in0=gu[cc][:], in1=gt[cc][:], op=ALU.mult)), reads=[b_gu[cc], b_gt[cc]], writes=[b_gu[cc]])
                        o_ap = out_fn(cc)
                        ini, bini = init_fn(cc)
                        if not rev:
                            P.op('dve', ('tensor_tensor_scan', dict(out=o_ap, data0=ga[cc][:], data1=gu[cc][:], initial=ini, op0=ALU.mult, op1=ALU.add)), reads=[b_ga[cc], b_gu[cc]] + bini, writes=[bout])
                        else:
                            P.op('dve', ('tensor_tensor_scan', dict(out=o_ap[:, ::-1], data0=ga[cc][:, ::-1], data1=gu[cc][:, ::-1], initial=ini, op0=ALU.mult, op1=ALU.add)), reads=[b_ga[cc], b_gu[cc]] + bini, writes=[bout])

                def load_xb(tb):
                    xb, bxb = xb_r.next()
                    lo = tb * TB - 2
                    hi = tb * TB + TB + 1
                    slo = max(lo, 0)
                    shi = min(hi, S)
                    if lo < 0:
                        P.op('pool', ('memset', dict(ap=xb[:, :, 0:2], constant=0.0)), writes=[bxb])
                    if hi > S:
                        P.op('pool', ('memset', dict(ap=xb[:, :, TB + 2:TB + 3], constant=0.0)), writes=[bxb])
                    P.dma('sp', ('dma_start', dict(out=xb[:, :, slo - lo:slo - lo + (shi - slo)], in_=xb_d.rearrange('(c p) t -> p c t', p=128)[:, :, slo:shi])), writes=[bxb])
                    return (xb, bxb)
                def conv(tb, xb, bxb):
                    tsl = slice(tb * TB, (tb + 1) * TB)
                    for cc in range(4):
                        P.op('dve', ('tensor_scalar', dict(out=ctmp[:, 0, :], in0=xb[:, cc, 0:TB], scalar1=col(l, 32 + 0 * 4 + cc), scalar2=col(l, 48 + cc), op0=ALU.mult, op1=ALU.add)), reads=[bxb, B_const], writes=[b_ctmp])
                        for j in (1, 2):
                            P.op('dve', ('scalar_tensor_tensor', dict(out=ctmp[:, 0, :], in0=xb[:, cc, j:j + TB], scalar=col(l, 32 + j * 4 + cc), in1=ctmp[:, 0, :], op0=ALU.mult, op1=ALU.add)), reads=[bxb, B_const, b_ctmp], writes=[b_ctmp])
                        P.op('dve', ('scalar_tensor_tensor', dict(out=cbf[:, cc, tsl], in0=xb[:, cc, 3:3 + TB], scalar=col(l, 32 + 3 * 4 + cc), in1=ctmp[:, 0, :], op0=ALU.mult, op1=ALU.add)), reads=[bxb, B_const, b_ctmp], writes=[b_cbf[tb]])
                conv(0, *load_xb(0))
                for tb in range(NTB):
                    if tb + 1 < NTB:
                        conv(tb + 1, *load_xb(tb + 1))
                    tsl = slice(tb * TB, (tb + 1) * TB)

                    def init_f(cc, tb=tb):
                        if tb == 0:
                            return (0.0, [])
                        return (hf[:, cc, tb * TB - 1:tb * TB], [b_hf[tb - 1]])
                    gates(0, tb, lambda cc, tsl=tsl: hf[:, cc, tsl], b_hf[tb], init_f, False)

                def tail(tb, hb, bhb, gg, bgg):
                    tsl = slice(tb * TB, (tb + 1) * TB)
                    P.op('dve', ('tensor_tensor', dict(out=yy[:], in0=hf[:, :, tsl], in1=hb[:], op=ALU.add)), reads=[b_hf[tb], bhb], writes=[b_yy])
                    P.op('pool', ('tensor_tensor', dict(out=yy[:], in0=yy[:], in1=gg[:], op=ALU.mult)), reads=[b_yy, bgg], writes=[b_yy])
                    rT, brT = rT_r.next()
                    rmsnorm_fm(yy, b_yy, 4, TB, ysq, b_ysq, ssb, b_ssb, lnv, b_lnv, rstd, b_rstd, rT, brT, lambda c: col(l, 76 + c), 1.0 / 512.0)
                    P.dma('pool', ('dma_start', dict(out=mix_d.rearrange('(c p) t -> p c t', p=128)[:, 4:8, tb * TB:(tb + 1) * TB], in_=rT[:])), reads=[brT])
                prev_hb = None
                pend = None
                for tb in range(NTB - 1, -1, -1):
                    hb, bhb = hb_r.next()
                    gg, bgg = gg_r.next()
                    P.dma('sp', ('dma_start', dict(out=gg[:], in_=gg_d.rearrange('(c p) t -> p c t', p=128)[:, :, tb * TB:(tb + 1) * TB])), writes=[bgg])

                    def init_b(cc, prev_hb=prev_hb):
                        if prev_hb is None:
                            return (0.0, [])
                        return (prev_hb[0][:, cc, 0:1], [prev_hb[1]])
                    gates(1, tb, lambda cc, hb=hb: hb[:, cc, :], bhb, init_b, True)
                    prev_hb = (hb, bhb)
                    if pend is not None:
                        tail(*pend)
                    pend = (tb, hb, bhb, gg, bgg)
                tail(*pend)
            P.barrier()
            with ExitStack() as st:
                wout = sb(st, 'd_wout', [128, KC, D], BF16)
                pass
                pass
                b_wout, b_wq, b_wo = (Buf(), Buf(), Buf())
                kxT = sb(st, 'd_kxT', [128, KC, NMEM], BF16)
                vx = sb(st, 'd_vx', [128, 2, D], BF16)
                b_kx, b_vx = (Buf(), Buf())
                mix_r = Ring([sb(st, 'd_mix%d' % i, [128, KC, TB], BF16) for i in range(2)])
                xin_r = Ring([sb(st, 'd_xin%d' % i, [128, KC, TB], F32) for i in range(2)])
                pass
                sq = sb(st, 'd_sq', [128, KC, TB], BF16)
                b_sq = Buf()
                qx = sb(st, 'd_qx', [128, KC, TB], BF16)
                b_qx = [Buf() for _ in range(KC)]
                ox = sb(st, 'd_ox', [128, KC, TB], BF16)
                b_ox = [Buf() for _ in range(KC)]
                lnv = sb(st, 'd_lnv', [128, TB], F32)
                b_lnv = Buf()
                rstd = sb(st, 'd_rstd', [128, TB], F32)
                b_rstd = Buf()
                pt_r = Ring([sb(st, 'd_pt%d' % i, [128, 2, TB], BF16) for i in range(2)])
                lden = sb(st, 'd_lden', [128, TB], F32)
                b_lden = Buf()
                rden_r = Ring([sb(st, 'd_rden%d' % i, [128, TB], F32) for i in range(2)])
                main_r = Ring(banks[0:5])
                main_r.b = bbank[0:5]
                ssb, b_ssb = (banks[5], bbank[5])
                denb, b_denb = (banks[6], bbank[6])
                wq = sb(st, 'd_wq', [128, KC, D], BF16)
                wo = sb(st, 'd_wo', [128, KC, D], BF16)
                with ExitStack() as st2:
                    wk = sb(st2, 'd_wk', [128, KC, D], BF16)
                    wv = sb(st2, 'd_wv', [128, KC, D], BF16)
                    b_wk, b_wv = (Buf(), Buf())
                    memT = sb(st2, 'd_memT', [128, KC, NMEM], F32)
                    b_memT = Buf()
                    mT = sb(st2, 'd_mT', [128, KC, NMEM], BF16)
                    b_mT = Buf()
                    P.dma('sp', ('dma_start', dict(out=memT[:], in_=memT_d.rearrange('(c p) t -> p c t', p=128))), writes=[b_memT])
                    P.dma('pool', ('dma_start', dict(out=wk[:], in_=wk_d[l].rearrange('(c p) n -> p c n', p=128))), writes=[b_wk])
                    P.dma('pool', ('dma_start', dict(out=wv[:], in_=wv_d[l].rearrange('(c p) n -> p c n', p=128))), writes=[b_wv])
                    P.dma('pool', ('dma_start', dict(out=wout[:], in_=w_out_d[l].rearrange('(c p) n -> p c n', p=128))), writes=[b_wout])
                    P.dma('pool', ('dma_start', dict(out=wq[:], in_=wq_d[l].rearrange('(c p) n -> p c n', p=128))), writes=[b_wq])
                    P.dma('pool', ('dma_start', dict(out=wo[:], in_=wo_d[l].rearrange('(c p) n -> p c n', p=128))), writes=[b_wo])
                    pass
                    pass
                    rmsnorm_fm(memT, b_memT, KC, NMEM, sq, b_sq, ssb, b_ssb, lnv, b_lnv, rstd, b_rstd, mT, b_mT, lambda c: col(l, 16 + c), 1.0 / D)
                    for oc in range(KC):
                        bk, bbk = main_r.next()
                        for kc in range(KC):
                            P.op('pe', ('matmul', dict(out=bk[:, 0:NMEM], lhsT=wk[:, kc, oc * 128:(oc + 1) * 128], rhs=mT[:, kc, :], start=kc == 0, stop=kc == KC - 1)), reads=[b_wk, b_mT], writes=[bbk])
                        P.op('act', ('activation', dict(out=kxT[:, oc, :], in_=bk[:, 0:NMEM], func=AF.Copy)), reads=[bbk], writes=[b_kx])
                    for mb in range(2):
                        for half in range(2):
                            bk, bbk = main_r.next()
                            for kc in range(KC):
                                P.op('pe', ('matmul', dict(out=bk[:], lhsT=mT[:, kc, mb * 128:(mb + 1) * 128], rhs=wv[:, kc, half * 512:(half + 1) * 512], start=kc == 0, stop=kc == KC - 1)), reads=[b_wv, b_mT], writes=[bbk])
                            P.op('dve', ('tensor_copy', dict(out=vx[:, mb, half * 512:(half + 1) * 512], in_=bk[:])), reads=[bbk], writes=[b_vx])
                    P.barrier()

                def load_de(tb):
                    mix, bmix = mix_r.next()
                    xin, bxin = xin_r.next()
                    P.dma('sp', ('dma_start', dict(out=mix[:], in_=mix_d.rearrange('(c p) t -> p c t', p=128)[:, :, tb * TB:(tb + 1) * TB])), writes=[bmix])
                    P.dma('sp', ('dma_start', dict(out=xin[:], in_=xT_d.rearrange('(c p) t -> p c t', p=128)[:, :, tb * TB:(tb + 1) * TB])), writes=[bxin])
                    return (mix, bmix, xin, bxin)
                x1_r = Ring([sb(st, 'd_x1%d' % i, [128, KC, TB], F32) for i in range(2)])
                hq_r = Ring([sb(st, 'd_hq%d' % i, [128, KC, TB], BF16) for i in range(2)])

                def phase1(mix, bmix, xin, bxin):
                    x1, bx1 = x1_r.next()
                    for oc in range(KC):
                        bk, bbk = main_r.next()
                        for kc in range(KC):
                            P.op('pe', ('matmul', dict(out=bk[:], lhsT=wout[:, kc, oc * 128:(oc + 1) * 128], rhs=mix[:, kc, :], start=kc == 0, stop=kc == KC - 1)), reads=[b_wout, bmix], writes=[bbk])
                        P.op('dve', ('tensor_tensor', dict(out=x1[:, oc, :], in0=bk[:], in1=xin[:, oc, :], op=ALU.add)), reads=[bbk, bxin], writes=[bx1])
                    norm1(x1, bx1, KC, TB, sq, b_sq)
                    return (x1, bx1)

                def phase1b(x1, bx1):
                    hq, b_hq = hq_r.next()
                    norm2(x1, bx1, KC, TB, sq, b_sq, ssb, b_ssb, lnv, b_lnv, rstd, b_rstd, hq, b_hq, lambda c: col(l, 8 + c), 1.0 / D)
                    return (hq, b_hq)

                def qproj(hq, b_hq):
                    for oc in range(KC):
                        bk, bbk = main_r.next()
                        for kc in range(KC):
                            P.op('pe', ('matmul', dict(out=bk[:], lhsT=wq[:, kc, oc * 128:(oc + 1) * 128], rhs=hq[:, kc, :], start=kc == 0, stop=kc == KC - 1)), reads=[b_wq, b_hq], writes=[bbk])
                        P.op('act', ('activation', dict(out=qx[:, oc, :], in_=bk[:], func=AF.Copy)), reads=[bbk], writes=[b_qx[oc]])

                def scores(hh):
                    pt, bpt = pt_r.next()
                    for mb in range(2):
                        bk, bbk = main_r.next()
                        for dc in range(2):
                            P.op('pe', ('matmul', dict(out=bk[:], lhsT=kxT[:, 2 * hh + dc, mb * 128:(mb + 1) * 128], rhs=qx[:, 2 * hh + dc, :], start=dc == 0, stop=dc == 1)), reads=[b_kx, b_qx[2 * hh + dc]], writes=[bbk])
                        P.op('act', ('activation', dict(out=pt[:, mb, :], in_=bk[:], func=AF.Exp, scale=1.0 / 16.0)), reads=[bbk], writes=[bpt])
                    return (pt, bpt)

                def attn_tail(hh, pt, bpt):
                    for mb in range(2):
                        P.op('pe', ('matmul', dict(out=denb[:], lhsT=ones[:], rhs=pt[:, mb, :], start=mb == 0, stop=mb == 1)), reads=[bpt, B_const], writes=[b_denb])
                    rden, brden = rden_r.next()
                    P.op('act', ('activation', dict(out=lden[:], in_=denb[:], func=AF.Ln)), reads=[b_denb], writes=[b_lden])
                    P.op('act', ('activation', dict(out=rden[:], in_=lden[:], func=AF.Exp, scale=-1.0)), reads=[b_lden], writes=[brden])
                    for dc in range(2):
                        bk, bbk = main_r.next()
                        for mb in range(2):
                            P.op('pe', ('matmul', dict(out=bk[:], lhsT=vx[:, mb, (2 * hh + dc) * 128:(2 * hh + dc + 1) * 128], rhs=pt[:, mb, :], start=mb == 0, stop=mb == 1)), reads=[b_vx, bpt], writes=[bbk])
                        P.op('dve', ('tensor_tensor', dict(out=ox[:, 2 * hh + dc, :], in0=bk[:], in1=rden[:], op=ALU.mult)), reads=[bbk, brden], writes=[b_ox[2 * hh + dc]])

                def wo_phase(x1, bx1, tb):
                    for oc in range(KC):
                        bk, bbk = main_r.next()
                        for kc in range(KC):
                            P.op('pe', ('matmul', dict(out=bk[:], lhsT=wo[:, kc, oc * 128:(oc + 1) * 128], rhs=ox[:, kc, :], start=kc == 0, stop=kc == KC - 1)), reads=[b_wo, b_ox[kc]], writes=[bbk])
                        P.op('dve', ('tensor_tensor', dict(out=x1[:, oc, :], in0=bk[:], in1=x1[:, oc, :], op=ALU.add)), reads=[bbk, bx1], writes=[bx1])
                    P.dma('pool', ('dma_start', dict(out=xT_d.rearrange('(c p) t -> p c t', p=128)[:, :, tb * TB:(tb + 1) * TB], in_=x1[:])), reads=[bx1])
                ld = load_de(0)
                xcur = phase1(*ld)
                hcur = phase1b(*xcur)
                for tb in range(NTB):
                    more = tb + 1 < NTB
                    if more:
                        ldn = load_de(tb + 1)
                    qproj(*hcur)
                    if more:
                        xnx = phase1(*ldn)
                    ptc = scores(0)
                    for hh in range(4):
                        ptn = scores(hh + 1) if hh < 3 else None
                        attn_tail(hh, *ptc)
                        if hh == 1 and more:
                            hnx = phase1b(*xnx)
                        ptc = ptn
                    wo_phase(xcur[0], xcur[1], tb)
                    if more:
                        xcur = xnx
                        hcur = hnx
            P.barrier()
            with ExitStack() as st:
                w1 = sb(st, 'f_w1', [128, KC, 4 * D], BF16)
                w2 = sb(st, 'f_w2', [128, 32, D], BF16)
                b_w1 = [Buf() for _ in range(4)]
                b_w2 = [Buf() for _ in range(4)]
                xin_r = Ring([sb(st, 'f_xin%d' % i, [128, KC, TB], F32) for i in range(2)])
                pass
                b_sq = Buf()
                hm_r = Ring([sb(st, 'f_hm%d' % i, [128, KC, TB], BF16) for i in range(2)])
                act_r = Ring([sb(st, 'f_act%d' % i, [128, 8, TB], BF16) for i in range(2)])
                rl_r = Ring([sb(st, 'f_rl%d' % i, [128, TB], F32) for i in range(2)])
                lnv = sb(st, 'f_lnv', [128, TB], F32)
                b_lnv = Buf()
                rstd = sb(st, 'f_rstd', [128, TB], F32)
                b_rstd = Buf()
                main_r = Ring(banks[0:6])
                main_r.b = bbank[0:6]
                ssb, b_ssb = (banks[6], bbank[6])
                for g in range(4):
                    P.dma('pool', ('dma_start', dict(out=w1[:, :, g * 1024:(g + 1) * 1024], in_=w1_d[l].rearrange('(c p) n -> p c n', p=128)[:, :, g * 1024:(g + 1) * 1024])), writes=[b_w1[g]])
                    P.dma('pool', ('dma_start', dict(out=w2[:, g * 8:(g + 1) * 8, :], in_=w2_d[l].rearrange('(c p) n -> p c n', p=128)[:, g * 8:(g + 1) * 8, :])), writes=[b_w2[g]])

                def load_f(tb):
                    xin, bxin = xin_r.next()
                    P.dma('sp', ('dma_start', dict(out=xin[:], in_=xT_d.rearrange('(c p) t -> p c t', p=128)[:, :, tb * TB:(tb + 1) * TB])), writes=[bxin])
                    return (xin, bxin)
                cur = load_f(0)
                hcur = hm_r.next()
                sqc = act_r.next()
                norm1(cur[0], cur[1], KC, TB, sqc[0], sqc[1])
                norm2(cur[0], cur[1], KC, TB, sqc[0], sqc[1], ssb, b_ssb, lnv, b_lnv, rstd, b_rstd, hcur[0], hcur[1], lambda c: col(l, 24 + c), 1.0 / D)
                for tb in range(NTB):
                    xin, bxin = cur
                    hm, b_hm = hcur
                    if tb + 1 < NTB:
                        nxt = load_f(tb + 1)
                        hnx = hm_r.next()
                    for g in range(4):
                        at, bat = act_r.next()
                        for j in range(8):
                            fc = g * 8 + j
                            bk, bbk = main_r.next()
                            rl, brl = rl_r.next()
                            for kc in range(KC):
                                P.op('pe', ('matmul', dict(out=bk[:], lhsT=w1[:, kc, fc * 128:(fc + 1) * 128], rhs=hm[:, kc, :], start=kc == 0, stop=kc == KC - 1)), reads=[b_w1[g], b_hm], writes=[bbk])
                            P.op('act', ('activation', dict(out=rl[:], in_=bk[:], func=AF.Relu)), reads=[bbk], writes=[brl])
                            P.op('dve', ('tensor_tensor', dict(out=at[:, j, :], in0=bk[:], in1=rl[:], op=ALU.mult)), reads=[bbk, brl], writes=[bat])
                        if g == 1 and tb + 1 < NTB:
                            sqn = act_r.next()
                            norm1(nxt[0], nxt[1], KC, TB, sqn[0], sqn[1])
                        for oc in range(KC):
                            bk, bbk = main_r.next()
                            for j in range(8):
                                P.op('pe', ('matmul', dict(out=bk[:], lhsT=w2[:, g * 8 + j, oc * 128:(oc + 1) * 128], rhs=at[:, j, :], start=j == 0, stop=j == 7)), reads=[b_w2[g], bat], writes=[bbk])
                            P.op('dve', ('tensor_tensor', dict(out=xin[:, oc, :], in0=bk[:], in1=xin[:, oc, :], op=ALU.add)), reads=[bbk, bxin], writes=[bxin])
                        if g == 2 and tb + 1 < NTB:
                            norm2(nxt[0], nxt[1], KC, TB, sqn[0], sqn[1], ssb, b_ssb, lnv, b_lnv, rstd, b_rstd, hnx[0], hnx[1], lambda c: col(l, 24 + c), 1.0 / D)
                    if tb + 1 < NTB:
                        cur = nxt
                        hcur = hnx
                    P.dma('pool', ('dma_start', dict(out=xT_d.rearrange('(c p) t -> p c t', p=128)[:, :, tb * TB:(tb + 1) * TB], in_=xin[:])), reads=[bxin])
            P.barrier()
        with ExitStack() as st:
            xin_r = Ring([sb(st, 'g_xin%d' % i, [128, KC, TB], F32) for i in range(2)])
            sq = sb(st, 'g_sq', [128, KC, TB], BF16)
            b_sq = Buf()
            yT_r = Ring([sb(st, 'g_yT%d' % i, [128, KC, TB], F32) for i in range(2)])
            lnv = sb(st, 'g_lnv', [128, TB], F32)
            b_lnv = Buf()
            rstd = sb(st, 'g_rstd', [128, TB], F32)
            b_rstd = Buf()
            ot_r = Ring([sb(st, 'g_ot%d' % i, [128, D], F32) for i in range(3)])
            main_r = Ring(banks[0:6])
            main_r.b = bbank[0:6]
            ssb, b_ssb = (banks[6], bbank[6])
            b_out = []
            for tb in range(NTB):
                xin, bxin = xin_r.next()
                P.dma('sp', ('dma_start', dict(out=xin[:], in_=xT_d.rearrange('(c p) t -> p c t', p=128)[:, :, tb * TB:(tb + 1) * TB])), writes=[bxin])
                yT, byT = yT_r.next()
                rmsnorm_fm(xin, bxin, KC, TB, sq, b_sq, ssb, b_ssb, lnv, b_lnv, rstd, b_rstd, yT, byT, lambda c: col(0, 80 + c), 1.0 / D)
                for ts in range(4):
                    ot, bot = ot_r.next()
                    for half in range(2):
                        bk, bbk = main_r.next()
                        for j in range(4):
                            c = half * 4 + j
                            P.op('pe', ('transpose', dict(out=bk[:, j * 128:(j + 1) * 128], in_=yT[:, c, ts * 128:(ts + 1) * 128], identity=ident[:])), reads=[byT, B_const], writes=[bbk])
                        if half == 0:
                            P.op('dve', ('tensor_copy', dict(out=ot[:, 0:512], in_=bk[:])), reads=[bbk], writes=[bot])
                        else:
                            P.op('act', ('activation', dict(out=ot[:, 512:1024], in_=bk[:], func=AF.Copy)), reads=[bbk], writes=[bot])
                    bo = Buf()
                    b_out.append(bo)
                    r0 = tb * TB + ts * 128
                    P.dma('pool', ('dma_start', dict(out=out_d[r0:r0 + 128, :], in_=ot[:])), reads=[bot], writes=[bo])
            P.op('sp', None, reads=b_out)
        P.barrier()
        P.emit()
    return (nc, P)

def _colify(v):
    v = np.asarray(v, np.float32).reshape(-1, 128)
    return np.ascontiguousarray(v.T)

def prep_shared(inp, depth):
    f = lambda a: np.ascontiguousarray(np.asarray(a, np.float32))
    sh = {}
    for k in ('w_in', 'w_out', 'xa_wq', 'xa_wk', 'xa_wv', 'xa_wo', 'mlp_w1', 'mlp_w2'):
        sh[k] = f(inp[k])[:depth]
    cols = np.zeros((depth, 128, NCOL), np.float32)
    bd = np.zeros((depth, 128, 2, 2, 4, 128), np.float32)
    for l in range(depth):
        cols[l, :, 0:8] = _colify(inp['g_mix'][l])
        cols[l, :, 8:16] = _colify(inp['g_xq'][l])
        cols[l, :, 16:24] = _colify(inp['g_mem'][l])
        cols[l, :, 24:32] = _colify(inp['g_mlp'][l])
        cw = np.asarray(inp['lru_conv_w'][l], np.float32)
        for j in range(4):
            cols[l, :, 32 + j * 4:36 + j * 4] = _colify(cw[j])
        cols[l, :, 48:52] = _colify(inp['lru_conv_b'][l])
        for dr in range(2):
            cols[l, :, 52 + dr * 4:56 + dr * 4] = _colify(np.asarray(inp['lru_b_r'][l, dr]).reshape(-1))
            cols[l, :, 60 + dr * 4:64 + dr * 4] = _colify(np.asarray(inp['lru_b_i'][l, dr]).reshape(-1))
            cols[l, :, 68 + dr * 4:72 + dr * 4] = _colify(inp['lru_a_param'][l, dr])
        cols[l, :, 76:80] = _colify(inp['lru_norm_g'][l])
        cols[l, :, 80:88] = _colify(inp['g_final'])
        for dr in range(2):
            for g, key in enumerate(('lru_w_r', 'lru_w_i')):
                w = np.asarray(inp[key][l, dr], np.float32)
                for n in range(NB):
                    cc, half = (n // 2, n % 2)
                    bd[l, half * 64:(half + 1) * 64, dr, g, cc, half * 64:(half + 1) * 64] = w[n]
    sh['cols'] = cols
    sh['lru_bd'] = np.ascontiguousarray(bd.reshape(depth, 128, 16 * 128))
    lamv = np.stack([np.concatenate([np.asarray(inp[k][l], np.float32) for k in ('da_lq1', 'da_lk1', 'da_lq2', 'da_lk2')]) for l in range(depth)])
    sh['lamv'] = np.ascontiguousarray(lamv.reshape(depth, 1, 256))
    sh['gsub'] = np.ascontiguousarray(np.asarray(inp['da_subln_g'], np.float32)[:depth].reshape(depth, 128, 1))
    sh['ident'] = np.eye(128, dtype=np.float32)
    pm = np.zeros((128, 128), np.float32)
    fsc = np.zeros((128, 3), np.float32)
    inv_freq = np.power(np.float32(ROPE_THETA), -np.arange(0, 16, 2, dtype=np.float32) / np.float32(16)).astype(np.float32)
    for m in range(128):
        d = m % 64
        if d < 8:
            pm[m + 8, m] = 1.0
            fsc[m, 0] = inv_freq[d]
            fsc[m, 1] = -1.0
        elif d < 16:
            pm[m - 8, m] = 1.0
            fsc[m, 0] = inv_freq[d - 8]
            fsc[m, 1] = 1.0
    for m in range(128):
        fsc[m, 2] = inv_freq[m // 16]
    sh['pmat'] = pm
    sh['fscol'] = fsc
    return sh
_CACHE = {}

def kernel(**inputs):
    x = np.asarray(inputs['x'], np.float32)
    B, S, _ = x.shape
    depth = np.asarray(inputs['w_in']).shape[0]
    key = (S, depth)
    if key not in _CACHE:
        _CACHE[key] = build(S, depth, False)[0]
    nc = _CACHE[key]
    sh = prep_shared(inputs, depth)
    mem = np.asarray(inputs['mem'], np.float32)
    pos = np.asarray(inputs['positions'], np.int32)
    in_maps = []
    for b in range(B):
        m = dict(sh)
        m['x'] = np.ascontiguousarray(x[b])
        m['mem'] = np.ascontiguousarray(mem[b])
        m['pos'] = np.ascontiguousarray(pos[b].reshape(1, S))
        in_maps.append(m)
    res = run_bass_kernel_spmd(nc, in_maps, core_ids=list(range(B)))
    return np.stack([np.asarray(r['out'], np.float32) for r in res.results], axis=0)
```
